# Optimizing a Trainium2 kernel written in Bass

```python
import jax, jax.numpy as jnp
from jax import lax
import numpy as np

D_MODEL = 2048
BATCH = 4
SEQ = 2048
DEPTH = 1
DEC_BATCH = 128
DEC_SEQ = 4
PAST_LEN = 16384
PAGE_SIZE = 128

N_HEADS_A = 4
DK_A = 256
DV_A = 512
QK_A = N_HEADS_A * DK_A
V_A = N_HEADS_A * DV_A
CONV_W = 4
MLSTM_CHUNK = 64
N_GROUPS_B = 4
D_B = 2048
GMLP_CHUNK = 128
D_FF = 5632
D_PLE = 256
EPS = 1e-6
D_IN = 2 * QK_A + 2 * V_A + 2 * N_HEADS_A + 2 * D_B + 2 * D_MODEL

kernel_name = 'hybrid_mlstm_gmlp_decoder_step'


def _rms(x, g):
    xf = x.astype(jnp.float32)
    y = xf * lax.rsqrt(jnp.mean(xf * xf, axis=-1, keepdims=True) + EPS)
    return (y * g.astype(jnp.float32)).astype(x.dtype)


def _layer_norm(x, g, b):
    xf = x.astype(jnp.float32)
    mu = jnp.mean(xf, axis=-1, keepdims=True)
    var = jnp.mean(jnp.square(xf - mu), axis=-1, keepdims=True)
    y = (xf - mu) * lax.rsqrt(var + EPS) * g.astype(jnp.float32) + b.astype(jnp.float32)
    return y.astype(x.dtype)


def _swiglu(x, wg, wu, wd):
    return (jax.nn.silu(x @ wg) * (x @ wu)) @ wd


def _causal_conv(x, buf, w, b):
    S = x.shape[1]
    xp = jnp.concatenate([buf.astype(x.dtype), x], axis=1)
    y = b
    for j in range(CONV_W):
        y = y + xp[:, j:j + S] * w[j]
    return y, xp[:, S:]


def _mlstm(q, k, v, ig, lf, C0, n0, m0):
    B, S, H, _ = q.shape
    L = min(S, MLSTM_CHUNK)
    nc = S // L

    def to_chunks(a):
        return jnp.moveaxis(a.reshape((B, nc, L) + a.shape[2:]), 1, 0)

    xs = (to_chunks(q), to_chunks(k), to_chunks(v), to_chunks(ig), to_chunks(lf))
    mask = jnp.tril(jnp.ones((L, L), dtype=bool))

    def step(carry, inp):
        C, n, m = carry
        qc, kc, vc, ic, fc = inp
        qc = qc.transpose(0, 2, 1, 3)
        kc = kc.transpose(0, 2, 1, 3)
        vc = vc.transpose(0, 2, 1, 3)
        ic = ic.transpose(0, 2, 1)
        bcum = jnp.cumsum(fc.transpose(0, 2, 1), axis=-1)
        d = jnp.where(mask, bcum[..., :, None] - bcum[..., None, :] + ic[..., None, :], -jnp.inf)
        m_in = bcum + m[..., None]
        m_t = jnp.maximum(m_in, jnp.max(d, axis=-1))
        s = jnp.einsum('bhtd,bhsd->bhts', qc, kc) * jnp.exp(d - m_t[..., None])
        w_prev = jnp.exp(m_in - m_t)
        num = jnp.einsum('bhts,bhsv->bhtv', s, vc) + w_prev[..., None] * jnp.einsum('bhtd,bhdv->bhtv', qc, C)
        den = jnp.sum(s, axis=-1) + w_prev * jnp.einsum('bhtd,bhd->bht', qc, n)
        h = num / jnp.maximum(jnp.abs(den), jnp.exp(-m_t))[..., None]
        m_new = m_t[..., -1]
        w_end = jnp.exp(bcum[..., -1:] - bcum + ic - m_new[..., None])
        decay = jnp.exp(bcum[..., -1] + m - m_new)
        C_new = decay[..., None, None] * C + jnp.einsum('bhs,bhsd,bhsv->bhdv', w_end, kc, vc)
        n_new = decay[..., None] * n + jnp.einsum('bhs,bhsd->bhd', w_end, kc)
        return (C_new, n_new, m_new), h

    (C, n, m), hs = lax.scan(step, (C0, n0, m0), xs)
    h = jnp.moveaxis(hs, 0, 1).transpose(0, 1, 3, 2, 4).reshape(B, S, H, -1)
    return h, C, n, m


def _spatial_gate(v, w_s, b_s):
    B, S, _ = v.shape
    L = min(S, GMLP_CHUNK)
    vc = v.reshape(B, S // L, L, N_GROUPS_B, D_B // N_GROUPS_B)
    w = jnp.where(jnp.tril(jnp.ones((L, L), dtype=bool)), w_s[:, :L, :L], 0.0).astype(v.dtype)
    out = jnp.einsum('gts,bnsgc->bntgc', w, vc) + b_s[:, :L].T[None, None, :, :, None].astype(v.dtype)
    return out.reshape(B, S, D_B)


def _layer(x, p, conv_buf, C0, n0, m0, lw):
    B, S, _ = x.shape
    f32 = jnp.float32
    h = x + 0.5 * _rms(_swiglu(_rms(x, lw['g_ffn1_pre']), lw['w_ffn1_gate'], lw['w_ffn1_up'], lw['w_ffn1_down']), lw['g_ffn1_post'])
    xn = _rms(h, lw['g_mix_pre'])
    z = xn @ lw['w_in']
    sizes = [2 * QK_A, V_A, V_A, N_HEADS_A, N_HEADS_A, D_B, D_B, D_MODEL, D_MODEL]
    qk_pre, v_a, o_a, i_pre, f_pre, u_b, v_b, gate_a, gate_b = jnp.split(z, [int(c) for c in np.cumsum(sizes)[:-1]], axis=-1)
    qk, new_buf = _causal_conv(qk_pre, conv_buf, lw['w_conv'], lw['b_conv'])
    qk = jax.nn.silu(qk).astype(f32)
    q = qk[..., :QK_A].reshape(B, S, N_HEADS_A, DK_A)
    k = qk[..., QK_A:].reshape(B, S, N_HEADS_A, DK_A) * (DK_A ** -0.5)
    v = v_a.astype(f32).reshape(B, S, N_HEADS_A, DV_A)
    ig = i_pre.astype(f32) + lw['b_igate'].astype(f32)
    lf = jax.nn.log_sigmoid(f_pre.astype(f32) + lw['b_fgate'].astype(f32))
    h_a, C, n, m = _mlstm(q, k, v, ig, lf, C0.astype(f32), n0.astype(f32), m0.astype(f32))
    h_a = h_a * lax.rsqrt(jnp.mean(h_a * h_a, axis=-1, keepdims=True) + EPS) * lw['g_head'].astype(f32).reshape(N_HEADS_A, DV_A)
    h_a = (jax.nn.sigmoid(o_a.astype(f32)) * h_a.reshape(B, S, V_A)).astype(x.dtype)
    y_a = h_a @ lw['w_a_out']
    u = jax.nn.gelu(u_b, approximate=False)
    vg = _layer_norm(jax.nn.gelu(v_b, approximate=False), lw['g_ln_v'], lw['b_ln_v'])
    y_b = (u * _spatial_gate(vg, lw['w_spatial'], lw['b_spatial'])) @ lw['w_b_out']
    mix = (jax.nn.sigmoid(gate_a) * y_a + jax.nn.sigmoid(gate_b) * y_b) @ lw['w_o']
    h = h + _rms(mix, lw['g_mix_post'])
    h = h + 0.5 * _rms(_swiglu(_rms(h, lw['g_ffn2_pre']), lw['w_ffn2_gate'], lw['w_ffn2_up'], lw['w_ffn2_down']), lw['g_ffn2_post'])
    e = jax.nn.sigmoid(_rms(h, lw['g_ple_pre']) @ lw['w_ple_gate']) * (p @ lw['w_ple_up'])
    h = h + _rms(e, lw['g_ple_post'])
    return h, new_buf, C, n, m, vg


def setup_inputs(seed: int = 0) -> dict:
    key = jax.random.key(seed)
    ks = list(jax.random.split(key, 48))

    def nrm(shape, scale):
        return jax.random.normal(ks.pop(), shape, jnp.float32) * scale

    def gain(width):
        return 1.0 + nrm((DEPTH, width), 0.05)

    return {
        'x_prompt': nrm((BATCH, SEQ, D_MODEL), 1.0),
        'x_sample': nrm((DEC_BATCH, DEC_SEQ, D_MODEL), 1.0),
        'p_prompt': nrm((DEPTH, BATCH, SEQ, D_PLE), 1.0),
        'p_sample': nrm((DEPTH, DEC_BATCH, DEC_SEQ, D_PLE), 1.0),
        'state_mlstm_conv': nrm((DEPTH, DEC_BATCH, CONV_W - 1, 2 * QK_A), 1.0),
        'state_mlstm_C': nrm((DEPTH, DEC_BATCH, N_HEADS_A, DK_A, DV_A), 0.02),
        'state_mlstm_n': nrm((DEPTH, DEC_BATCH, N_HEADS_A, DK_A), 0.1),
        'state_mlstm_m': nrm((DEPTH, DEC_BATCH, N_HEADS_A), 1.0),
        'g_ffn1_pre': gain(D_MODEL),
        'w_ffn1_gate': nrm((DEPTH, D_MODEL, D_FF), D_MODEL ** -0.5),
        'w_ffn1_up': nrm((DEPTH, D_MODEL, D_FF), D_MODEL ** -0.5),
        'w_ffn1_down': nrm((DEPTH, D_FF, D_MODEL), D_FF ** -0.5),
        'g_ffn1_post': gain(D_MODEL),
        'g_mix_pre': gain(D_MODEL),
        'w_in': nrm((DEPTH, D_MODEL, D_IN), D_MODEL ** -0.5),
        'w_conv': nrm((DEPTH, CONV_W, 2 * QK_A), CONV_W ** -0.5),
        'b_conv': nrm((DEPTH, 2 * QK_A), 0.02),
        'b_igate': nrm((DEPTH, N_HEADS_A), 0.1),
        'b_fgate': 3.0 + nrm((DEPTH, N_HEADS_A), 0.5),
        'g_head': gain(V_A),
        'w_a_out': nrm((DEPTH, V_A, D_MODEL), V_A ** -0.5),
        'g_ln_v': gain(D_B),
        'b_ln_v': nrm((DEPTH, D_B), 0.02),
        'w_spatial': nrm((DEPTH, N_GROUPS_B, GMLP_CHUNK, GMLP_CHUNK), GMLP_CHUNK ** -0.5),
        'b_spatial': 1.0 + nrm((DEPTH, N_GROUPS_B, GMLP_CHUNK), 0.05),
        'w_b_out': nrm((DEPTH, D_B, D_MODEL), D_B ** -0.5),
        'w_o': nrm((DEPTH, D_MODEL, D_MODEL), D_MODEL ** -0.5),
        'g_mix_post': gain(D_MODEL),
        'g_ffn2_pre': gain(D_MODEL),
        'w_ffn2_gate': nrm((DEPTH, D_MODEL, D_FF), D_MODEL ** -0.5),
        'w_ffn2_up': nrm((DEPTH, D_MODEL, D_FF), D_MODEL ** -0.5),
        'w_ffn2_down': nrm((DEPTH, D_FF, D_MODEL), D_FF ** -0.5),
        'g_ffn2_post': gain(D_MODEL),
        'g_ple_pre': gain(D_MODEL),
        'w_ple_gate': nrm((DEPTH, D_MODEL, D_MODEL), D_MODEL ** -0.5),
        'w_ple_up': nrm((DEPTH, D_PLE, D_MODEL), D_PLE ** -0.5),
        'g_ple_post': gain(D_MODEL),
    }


def reference(x_prompt, x_sample, p_prompt, p_sample, state_mlstm_conv, state_mlstm_C, state_mlstm_n, state_mlstm_m,
              g_ffn1_pre, w_ffn1_gate, w_ffn1_up, w_ffn1_down, g_ffn1_post,
              g_mix_pre, w_in, w_conv, b_conv, b_igate, b_fgate, g_head, w_a_out,
              g_ln_v, b_ln_v, w_spatial, b_spatial, w_b_out, w_o, g_mix_post,
              g_ffn2_pre, w_ffn2_gate, w_ffn2_up, w_ffn2_down, g_ffn2_post,
              g_ple_pre, w_ple_gate, w_ple_up, g_ple_post):
    hp, hs = x_prompt, x_sample
    conv_p, C_p, n_p, m_p = [], [], [], []
    conv_s, C_s, n_s, m_s, v_s = [], [], [], [], []
    for i in range(DEPTH):
        lw = dict(g_ffn1_pre=g_ffn1_pre[i], w_ffn1_gate=w_ffn1_gate[i], w_ffn1_up=w_ffn1_up[i], w_ffn1_down=w_ffn1_down[i],
                  g_ffn1_post=g_ffn1_post[i], g_mix_pre=g_mix_pre[i], w_in=w_in[i], w_conv=w_conv[i], b_conv=b_conv[i],
                  b_igate=b_igate[i], b_fgate=b_fgate[i], g_head=g_head[i], w_a_out=w_a_out[i], g_ln_v=g_ln_v[i],
                  b_ln_v=b_ln_v[i], w_spatial=w_spatial[i], b_spatial=b_spatial[i], w_b_out=w_b_out[i], w_o=w_o[i],
                  g_mix_post=g_mix_post[i], g_ffn2_pre=g_ffn2_pre[i], w_ffn2_gate=w_ffn2_gate[i], w_ffn2_up=w_ffn2_up[i],
                  w_ffn2_down=w_ffn2_down[i], g_ffn2_post=g_ffn2_post[i], g_ple_pre=g_ple_pre[i],
                  w_ple_gate=w_ple_gate[i], w_ple_up=w_ple_up[i], g_ple_post=g_ple_post[i])
        buf0 = jnp.zeros((hp.shape[0], CONV_W - 1, 2 * QK_A), hp.dtype)
        C0 = jnp.zeros((hp.shape[0], N_HEADS_A, DK_A, DV_A), jnp.float32)
        n0 = jnp.zeros((hp.shape[0], N_HEADS_A, DK_A), jnp.float32)
        m0 = jnp.zeros((hp.shape[0], N_HEADS_A), jnp.float32)
        hp, bp, cp, np_, mp, _ = _layer(hp, p_prompt[i], buf0, C0, n0, m0, lw)
        conv_p.append(bp); C_p.append(cp); n_p.append(np_); m_p.append(mp)
        hs, bs, cs, ns, ms, vs = _layer(hs, p_sample[i], state_mlstm_conv[i], state_mlstm_C[i], state_mlstm_n[i], state_mlstm_m[i], lw)
        conv_s.append(bs); C_s.append(cs); n_s.append(ns); m_s.append(ms); v_s.append(vs)
    return (hp, hs, jnp.stack(conv_p), jnp.stack(C_p), jnp.stack(n_p), jnp.stack(m_p),
            jnp.stack(conv_s), jnp.stack(C_s), jnp.stack(n_s), jnp.stack(m_s), jnp.stack(v_s))
```

```python
import numpy as np
import concourse.bass as bass
import concourse.mybir as mybir

F32 = mybir.dt.float32
BF16 = mybir.dt.bfloat16
I32 = mybir.dt.int32
AF = mybir.ActivationFunctionType
ALU = mybir.AluOpType
AX = mybir.AxisListType

ENGS = ("pe", "act", "dve", "pool", "sp")
SEM_LIMIT = 3000
DMA_POOL = 6
DMA_LIMIT = 180


def _region(ap):
    t = ap.tensor
    shp = tuple(t.shape)
    row = 1
    for s in shp[1:]:
        row *= s
    off = ap.offset
    p0 = off // row
    f0 = off % row
    dims = list(ap.ap)
    pc = dims[0][1]
    pstep = dims[0][0]
    if pstep == 0:
        p1 = p0 + 1
    else:
        p1 = p0 + (pc - 1) * (pstep // row if pstep >= row else 1) + 1
    ext = 0
    for st, cnt in dims[1:]:
        ext += (cnt - 1) * abs(st)
    f1 = f0 + ext + 1
    if "psum" in str(ap.space).lower():
        bw = 2048 // mybir.dt.size(ap.dtype) if hasattr(mybir.dt, "size") else (512 if ap.dtype == F32 else 1024)
        f0 = (f0 // bw) * bw
        f1 = -(-f1 // bw) * bw
        p0, p1 = 0, 128
    return (p0, p1, f0, f1)


def _overlap(a, b):
    return a[0] < b[1] and b[0] < a[1] and a[2] < b[3] and b[2] < a[3]


def _covers(a, b):
    return a[0] <= b[0] and a[1] >= b[1] and a[2] <= b[2] and a[3] >= b[3]


class _Op:
    __slots__ = ("eng", "fn", "deps", "signal", "sigidx", "is_dma", "dma_slot",
                 "dma_val", "dma_prev", "idx", "name", "tag")


class Sched:
    def __init__(self, nc):
        self.nc = nc
        self.ops = []
        self.track = {}
        self.ndma = {e: 0 for e in ENGS}
        self.tag = ""

    def _is_tracked(self, ap):
        sp = str(ap.space).lower() if hasattr(ap, "space") else ""
        return ("sb" in sp) or ("psum" in sp) or ("state" in sp)

    def op(self, eng, fn, reads=(), writes=(), dma=False, name=None):
        o = _Op()
        o.eng = eng
        o.fn = fn
        o.idx = len(self.ops)
        o.signal = False
        o.sigidx = None
        o.is_dma = dma
        o.name = name
        o.tag = self.tag
        o.dma_prev = None
        deps = set()
        for ap in reads:
            if ap is None or not self._is_tracked(ap):
                continue
            reg = _region(ap)
            lst = self.track.setdefault(ap.tensor.name, [])
            is_ps = "psum" in str(ap.space).lower()
            for r, kind, oi in lst:
                if not _overlap(r, reg):
                    continue
                if kind == "W":
                    deps.add(oi)
                elif is_ps and self.ops[oi].eng != eng:
                    deps.add(oi)
            lst.append([reg, "R", o.idx])
        for ap in writes:
            if ap is None or not self._is_tracked(ap):
                continue
            reg = _region(ap)
            lst = self.track.setdefault(ap.tensor.name, [])
            keep = []
            for rec in lst:
                r, kind, oi = rec
                if oi == o.idx:
                    keep.append(rec)
                    continue
                if _overlap(r, reg):
                    deps.add(oi)
                    if _covers(reg, r):
                        continue
                keep.append(rec)
            keep.append([reg, "W", o.idx])
            self.track[ap.tensor.name] = keep
        deps.discard(o.idx)
        fdeps = []
        for d in deps:
            po = self.ops[d]
            if po.eng == eng and not po.is_dma and not dma and eng == "pe":
                continue
            fdeps.append(d)
        o.deps = fdeps
        if dma:
            k = self.ndma[eng]
            self.ndma[eng] = k + 1
            o.dma_slot = k
        self.ops.append(o)
        return o

    def dma(self, eng, out, in_, **kw):
        return self.op(eng, lambda e: e.dma_start(out=out, in_=in_, **kw),
                       reads=[in_], writes=[out], dma=True)

    def emit(self):
        nc = self.nc
        ops = self.ops
        for o in ops:
            for d in o.deps:
                ops[d].signal = True
        per_eng = {e: [] for e in ENGS}
        cnt = {e: 0 for e in ENGS}
        for o in ops:
            per_eng[o.eng].append(o)
            if o.is_dma:
                continue
            if o.signal:
                cnt[o.eng] += 1
                o.sigidx = cnt[o.eng]
        nsem = {e: max(1, -(-cnt[e] // SEM_LIMIT)) for e in ENGS}
        ndsem = {e: (DMA_POOL * max(1, -(-self.ndma[e] // (DMA_POOL * DMA_LIMIT))) if self.ndma[e] else 0)
                 for e in ENGS}
        import contextlib
        with contextlib.ExitStack() as st:
            sems = {e: [st.enter_context(nc.semaphore(f"s_{e}_{i}")) for i in range(nsem[e])] for e in ENGS}
            dsems = {e: [st.enter_context(nc.semaphore(f"d_{e}_{i}")) for i in range(ndsem[e])] for e in ENGS}
            self.nsems = sum(nsem.values()) + sum(ndsem.values())

            def dma_sem(o):
                k = o.dma_slot
                epoch = k // (DMA_POOL * DMA_LIMIT)
                j = k % DMA_POOL
                nth = (k % (DMA_POOL * DMA_LIMIT)) // DMA_POOL + 1
                return dsems[o.eng][epoch * DMA_POOL + j], 16 * nth

            def comp_sem(o):
                i = o.sigidx - 1
                return sems[o.eng][i // SEM_LIMIT], (i % SEM_LIMIT) + 1

            block = st.enter_context(nc.Block())
            last_dma_on_slot = {}

            def run(engname, handle):
                waited = {e: 0 for e in ENGS}
                dwaited = set()
                dma_hist = []
                for o in per_eng[engname]:
                    need = {}
                    for d in o.deps:
                        po = ops[d]
                        if po.is_dma:
                            if d not in dwaited:
                                dwaited.add(d)
                                s, v = dma_sem(po)
                                handle.wait_ge(s, v)
                        else:
                            if po.sigidx > waited[po.eng]:
                                need[po.eng] = max(need.get(po.eng, 0), po.sigidx)
                    for e, v in need.items():
                        waited[e] = v
                        i = v - 1
                        handle.wait_ge(sems[e][i // SEM_LIMIT], (i % SEM_LIMIT) + 1)
                    if o.is_dma:
                        k = o.dma_slot
                        if k >= DMA_POOL and (k % (DMA_POOL * DMA_LIMIT)) >= DMA_POOL:
                            prev = dma_hist[k - DMA_POOL]
                            if prev.idx not in dwaited:
                                dwaited.add(prev.idx)
                                s, v = dma_sem(prev)
                                handle.wait_ge(s, v)
                        dma_hist.append(o)
                        ins = o.fn(handle)
                        s, v = dma_sem(o)
                        ins.then_inc(s, 16)
                    else:
                        ins = o.fn(handle)
                        if o.signal:
                            s, v = comp_sem(o)
                            ins.then_inc(s, 1)
                if dma_hist:
                    seen = set()
                    for o in reversed(dma_hist):
                        s, v = dma_sem(o)
                        key = id(s)
                        if key in seen:
                            continue
                        seen.add(key)
                        handle.wait_ge(s, v)

            @block.sync
            def _(h):
                run("sp", h)

            @block.scalar
            def _(h):
                run("act", h)

            @block.vector
            def _(h):
                run("dve", h)

            @block.gpsimd
            def _(h):
                run("pool", h)

            @block.tensor
            def _(h):
                run("pe", h)

import contextlib
import math
from concourse.bass_utils import run_bass_kernel_spmd

D = 2048
KC = 16
DFF = 5632
FC = 44
TB = 576
TP = 512
NS = 64
EPS = 1e-6
O_QK, O_VA, O_OA, O_I, O_F, O_UB, O_VB, O_GA, O_GB = 0, 2048, 4096, 6144, 6148, 6152, 8200, 10248, 12296
D_IN = 14344
VN = ['g_ffn1_pre', 'g_ffn1_post', 'g_mix_pre', 'b_conv', 'w_conv0', 'w_conv1', 'w_conv2', 'w_conv3',
      'g_head', 'g_ln_v', 'b_ln_v', 'g_mix_post', 'g_ffn2_pre', 'g_ffn2_post', 'g_ple_pre', 'g_ple_post']
VI = {n: i for i, n in enumerate(VN)}

C_ID, C_TRILT, C_MASKS, C_E, C_SEL, C_SEQ, C_RST, C_CB = 0, 128, 256, 320, 384, 896, 912, 1488
CW = 1496


def make_consts():
    c = np.zeros((128, CW), np.float32)
    c[:, C_ID:C_ID + 128] = np.eye(128, dtype=np.float32)
    s = np.arange(128)[:, None]
    t = np.arange(128)[None, :]
    c[:, C_TRILT:C_TRILT + 128] = (s <= t).astype(np.float32)
    s6 = np.arange(64)[:, None]
    t6 = np.arange(64)[None, :]
    c[:64, C_MASKS:C_MASKS + 64] = ((s6 // 4 == t6 // 4) & (s6 <= t6)).astype(np.float32)
    c[:4, C_E:C_E + 64] = (np.arange(64)[None, :] % 4 == np.arange(4)[:, None]).astype(np.float32)
    for h in range(4):
        c[h, C_SEL + h * 128:C_SEL + (h + 1) * 128] = 1.0
    c[:64, C_SEQ:C_SEQ + 16] = (np.arange(64)[:, None] // 4 == np.arange(16)[None, :]).astype(np.float32)
    r = np.ones(TB, np.float32)
    r[0:512:128] = 0.0
    r[512:576:4] = 0.0
    c[:4, C_RST:C_RST + TB] = r[None, :]
    c[:, C_CB + 0] = EPS
    c[:, C_CB + 1] = -math.log(16.0)
    c[:, C_CB + 2] = 1.0
    c[:, C_CB + 3] = 0.0
    return c


class SX(Sched):
    def mm(self, out, lhsT, rhs, start=True, stop=True, sg=False):
        rd = [lhsT, rhs] + ([] if start else [out])
        if sg:
            return self.op("pe", lambda e: e.matmul(out, lhsT, rhs, start=start, stop=stop, skip_group_check=True),
                           reads=rd, writes=[out])
        return self.op("pe", lambda e: e.matmul(out, lhsT, rhs, start=start, stop=stop), reads=rd, writes=[out])

    def tr(self, out, in_, ident):
        return self.op("pe", lambda e: e.transpose(out, in_, ident), reads=[in_, ident], writes=[out])

    def act(self, out, in_, func, bias=None, scale=None, accum=None):
        kw = {}
        rd = [in_]
        wr = [out]
        if bias is not None:
            kw["bias"] = bias
            if not isinstance(bias, (int, float)):
                rd.append(bias)
        if scale is not None:
            kw["scale"] = scale
            if not isinstance(scale, (int, float)):
                rd.append(scale)
        if accum is not None:
            kw["accum_out"] = accum
            wr.append(accum)
        return self.op("act", lambda e: e.activation(out, in_, func, **kw), reads=rd, writes=wr)

    def tt(self, eng, out, a, b, op):
        return self.op(eng, lambda e: e.tensor_tensor(out, a, b, op), reads=[a, b], writes=[out])

    def ts(self, eng, out, a, s1, s2, op0, op1=None):
        rd = [a] + [x for x in (s1, s2) if x is not None and not isinstance(x, (int, float))]
        if op1 is None:
            return self.op(eng, lambda e: e.tensor_scalar(out, a, s1, None, op0), reads=rd, writes=[out])
        return self.op(eng, lambda e: e.tensor_scalar(out, a, s1, s2, op0, op1), reads=rd, writes=[out])

    def stt(self, eng, out, in0, scalar, in1, op0, op1):
        rd = [in0, in1] + ([] if isinstance(scalar, (int, float)) else [scalar])
        return self.op(eng, lambda e: e.scalar_tensor_tensor(out, in0, scalar, in1, op0, op1), reads=rd, writes=[out])

    def cp(self, eng, out, in_):
        if eng == "act":
            return self.op("act", lambda e: e.copy(out, in_), reads=[in_], writes=[out])
        return self.op(eng, lambda e: e.tensor_copy(out, in_), reads=[in_], writes=[out])

    def memset(self, eng, ap, val):
        return self.op(eng, lambda e: e.memset(ap, val), writes=[ap])

    def recip(self, out, in_):
        return self.op("dve", lambda e: e.reciprocal(out, in_), reads=[in_], writes=[out])

    def rmax(self, out, in_):
        return self.op("dve", lambda e: e.reduce_max(out, in_, AX.X), reads=[in_], writes=[out])

    def rsum(self, out, in_):
        return self.op("dve", lambda e: e.reduce_sum(out, in_, AX.X), reads=[in_], writes=[out])

    def scan(self, out, d0, d1, init, op0, op1):
        rd = [d0, d1] + ([] if isinstance(init, (int, float)) else [init])
        return self.op("dve", lambda e: e.tensor_tensor_scan(out, d0, d1, init, op0, op1), reads=rd, writes=[out])


class _Stop(Exception):
    pass


def build_program(dbg=False, stop_at=None):
    nc = bass.Bass("TRN2", target_bir_lowering=False)

    def din(name, shape, dt=F32):
        return nc.dram_tensor(name, list(shape), dt, kind="ExternalInput").ap()

    def dout(name, shape, dt=F32):
        return nc.dram_tensor(name, list(shape), dt, kind="ExternalOutput").ap()

    xpre = din("xpre", [1024, D])
    xmain = din("xmain", [1024, D])
    pmain = din("pmain", [1024, 256])
    xs = din("xs", [NS, D])
    pss = din("ps", [NS, 256])
    convs = din("convs", [48, D])
    Cs = din("Cs", [16, 4, 256, 512])
    nsd = din("ns", [16, 1024])
    msd = din("ms", [16, 4])
    maskc = din("maskc", [128, 1])
    vecs = din("vecs", [256, 128])
    vrow = din("vrow", [16, D])
    bif = din("bif", [4, 2])
    constd = din("consts", [128, CW])
    w_sp = din("w_spatial", [4, 128, 128])
    b_sp = din("b_spatial", [4, 128])
    W = {}
    for nm, shp in [("w_ffn1_gate", [D, DFF]), ("w_ffn1_up", [D, DFF]), ("w_ffn1_down", [DFF, D]),
                    ("w_in", [D, D_IN]), ("w_a_out", [D, D]), ("w_b_out", [D, D]), ("w_o", [D, D]),
                    ("w_ffn2_gate", [D, DFF]), ("w_ffn2_up", [D, DFF]), ("w_ffn2_down", [DFF, D]),
                    ("w_ple_gate", [D, D]), ("w_ple_up", [256, D])]:
        W[nm] = din(nm, shp)

    y_main = dout("y_main", [1024, D])
    y_s = dout("y_s", [NS, D])
    conv_p = dout("conv_p", [3, D])
    C_p = dout("C_p", [4, 256, 512])
    n_p = dout("n_p", [8, 128])
    m_p = dout("m_p", [4, 1])
    conv_s = dout("conv_s", [48, D])
    C_s = dout("C_s", [16, 4, 256, 512])
    n_s = dout("n_s", [16, 1024])
    m_s = dout("m_s", [16, 4])
    v_s = dout("v_s", [NS, D])

    st = contextlib.ExitStack()
    with st:
        def sb(name, shape, dt):
            return st.enter_context(nc.sbuf_tensor(name, list(shape), dt))

        hT = sb("hT", [128, KC * TB], F32)
        xnT = sb("xnT", [128, KC * TB], BF16)
        G = sb("G", [128, FC * TB], BF16)
        Cst = sb("Cst", [128, 4 * 2 * 512], F32)
        Nst = sb("Nst", [128, 8], F32)
        Mst = sb("Mst", [4, 1], F32)
        Cb = sb("Cb", [128, 2 * 1024], BF16)
        WR = [sb(f"WR{i}", [128, 4096], BF16) for i in range(3)]
        SCR = sb("SCR", [128, 4096], F32)
        CONST = sb("CONST", [128, CW], F32)
        VEC = sb("VEC", [128, 256], F32)
        identb = sb("identb", [128, 128], BF16)
        onesb = sb("onesb", [128, 128], BF16)
        WgT = sb("WgT", [128, 4 * 128], BF16)
        WsT = sb("WsT", [64, 4 * 64], BF16)
        RSTD = sb("RSTD", [128, TB], F32)
        SQ = [sb(f"SQ{i}", [128, TB], BF16) for i in range(3)]
        STM = [sb(f"STM{i}", [128, TB], BF16) for i in range(4)]
        FT = [sb(f"FT{i}", [128, TB], F32) for i in range(2)]
        XP = [sb(f"XP{i}", [128, 640], F32) for i in range(1)]
        HP = sb("HP", [128, KC * 3], F32)
        HS = sb("HS", [128, KC * 48], F32)
        THR = sb("THR", [4, TB], F32)
        WKT = sb("WKT", [128, 5 * 4], F32)
        DCB = sb("DCB", [128, 4 * 20], F32)
        GSM = sb("GSM", [4, 64], F32)
        MKC = sb("MKC", [128, 1], F32)
        BIF = sb("BIF", [4, 4], F32)
        LNS = sb("LNS", [128, 5 * 16], F32)
        PTb = sb("PTb", [128, 2 * 128], BF16)
        NREP = sb("NREP", [128, 2 * 2 * 128], BF16)
        MLS = sb("MLS", [128, 3 * 128], F32)
        SQV = sb("SQV", [128, 512], BF16)
        N0T = sb("N0T", [128, 8 * 16], F32)
        NNT = sb("NNT", [128, 8 * 16], F32)
        QN = sb("QN", [128, 2 * 64], BF16)
        KM = sb("KM", [64, 2 * 256], BF16)
        seqmb = sb("seqmb", [64, 16], BF16)
        PS = st.enter_context(nc.psum_tensor("PS", [128, 4096], F32))

        S = SX(nc)
        MUL, ADD, SUB, MAX = ALU.mult, ALU.add, ALU.subtract, ALU.max

        def bank(b):
            return PS[:, 512 * b:512 * b + 512]

        identf = CONST[:, C_ID:C_ID + 128]
        trilT = CONST[:, C_TRILT:C_TRILT + 128]
        maskS = CONST[0:64, C_MASKS:C_MASKS + 64]
        Emat = CONST[0:4, C_E:C_E + 64]
        seqmask = CONST[0:64, C_SEQ:C_SEQ + 16]
        rst = CONST[0:4, C_RST:C_RST + TB]
        cb_eps = CONST[:, C_CB:C_CB + 1]
        cb_nl16 = CONST[:, C_CB + 1:C_CB + 2]
        cb_one = CONST[:, C_CB + 2:C_CB + 3]

        def sel(h):
            return CONST[0:4, C_SEL + h * 128:C_SEL + (h + 1) * 128]

        def vec(name, c):
            i = VI[name] * 16 + c
            return VEC[:, i:i + 1]

        def hc(c):
            return hT[:, c * TB:(c + 1) * TB]

        def xc(c):
            return xnT[:, c * TB:(c + 1) * TB]

        def gc(c):
            return G[:, c * TB:(c + 1) * TB]

        S.dma("sp", CONST[:], constd)
        S.dma("sp", MKC[:], maskc)
        S.dma("sp", BIF[:, 0:2], bif)
        S.memset("dve", onesb[:], 1.0)
        S.cp("dve", identb[:], identf)
        S.cp("dve", seqmb[:], seqmask)
        S.memset("dve", Cst[:], 0.0)
        S.memset("dve", Nst[:], 0.0)
        S.memset("dve", Mst[:], 0.0)
        S.memset("dve", HP[:], 0.0)
        S.ts("dve", BIF[:, 2:3], BIF[:, 1:2], -1.0, None, MUL)
        for i in range(2):
            slot = SCR[:, i * 128:(i + 1) * 128]
            S.dma("sp", slot, vecs[i * 128:(i + 1) * 128, :])
            S.tr(bank(i)[:, 0:128], slot, identf)
            S.cp("dve", VEC[:, i * 128:(i + 1) * 128], bank(i)[:, 0:128])
        for g in range(4):
            wt = SCR[:, 512 + g * 128:512 + (g + 1) * 128]
            S.dma("sp", wt, w_sp[g])
            S.tr(bank(2 + g % 2)[:, 0:128], wt, identf)
            S.tt("dve", SCR[:, 1024 + g * 128:1024 + (g + 1) * 128], bank(2 + g % 2)[:, 0:128], trilT, MUL)
            S.cp("dve", WgT[:, g * 128:(g + 1) * 128], SCR[:, 1024 + g * 128:1024 + (g + 1) * 128])
            w4 = SCR[0:4, 1600 + g * 4:1600 + g * 4 + 4]
            S.dma("sp", w4, w_sp[g, 0:4, 0:4])
            m1 = bank(5)[0:4, g * 64:(g + 1) * 64]
            S.op("pe", lambda e, m1=m1, w4=w4: e.matmul(m1, w4, Emat, start=True, stop=True), reads=[w4, Emat], writes=[m1])
            m1s = SCR[0:4, 1700 + g * 64:1700 + (g + 1) * 64]
            S.cp("dve", m1s, m1)
            rt = bank(5)[0:64, 256 + g * 64:256 + (g + 1) * 64]
            S.op("pe", lambda e, rt=rt, m1s=m1s: e.matmul(rt, Emat, m1s, start=True, stop=True), reads=[Emat, m1s], writes=[rt])
            S.tt("dve", WsT[:, g * 64:(g + 1) * 64], rt, maskS, MUL)

        state = {"ring": 0, "win": 0, "sq": 0, "stm": 0, "ft": 0, "xp": 0, "xs": 0, "mb": 0, "ev": 0, "hv": 0, "cb": 0, "c0": 0, "psb": 0}

        def nxt(key, n):
            v = state[key]
            state[key] = (v + 1) % n
            return v

        def ev_eng():
            state["ev"] ^= 1
            return "act" if state["ev"] else "dve"

        def load_panel(w_ap, k0c, nk, col0, ncol):
            slot = WR[nxt("ring", 3)]
            src = w_ap[k0c * 128:(k0c + nk) * 128, col0:col0 + ncol].rearrange("(kc p) c -> p kc c", p=128)
            dst = slot[:, 0:nk * ncol].rearrange("p (kc c) -> p kc c", c=ncol)
            S.dma("pool", dst, src)
            return dst

        def fm_group(w_ap, K, col0, ncol, xin, segs, mrows=128):
            nkc = K // 128
            nkp = 16 if nkc == 16 else (11 if nkc == 44 else nkc)
            nm = max(1, ncol // 128)
            mw = min(128, ncol)
            wb = [1024 * nxt("win", 3) for _ in range(nm)]
            ttot = sum(n for _, n in segs)
            for p0 in range(0, nkc, nkp):
                pan = load_panel(w_ap, p0, nkp, col0, ncol)
                for mi in range(nm):
                    for kl in range(nkp):
                        kc = p0 + kl
                        for (c0, n) in segs:
                            o = PS[0:mrows if ncol >= 128 else mw, wb[mi] + c0:wb[mi] + c0 + n]
                            S.mm(o, pan[:, kl, mi * mw:(mi + 1) * mw], xin(kc)[:, c0:c0 + n],
                                 start=(kc == 0), stop=(kc == nkc - 1))
            return [PS[0:(128 if ncol >= 128 else mw), b:b + ttot] for b in wb]

        def tm_group(w_ap, col0, blocks, consume):
            for p0 in (0, 8):
                pan = load_panel(w_ap, p0, 8, col0, 512)
                for bi, (c0, n) in enumerate(blocks):
                    for kl in range(8):
                        kc = p0 + kl
                        S.mm(PS[0:n, 512 * bi:512 * bi + 512], xc(kc)[:, c0:c0 + n], pan[:, kl, :],
                             start=(kc == 0), stop=(kc == 15))
            for bi, (c0, n) in enumerate(blocks):
                consume(bi, n, PS[0:n, 512 * bi:512 * bi + 512])

        STAT = PS[:, 3072:3072 + TB]

        def stats_mm(sq, segs, first, last):
            for (c0, n) in segs:
                S.mm(STAT[:, c0:c0 + n], onesb[:], sq[:, c0:c0 + n], start=first, stop=last)

        def finish_rstd(T, dim):
            ft = FT[nxt("ft", 2)]
            S.act(ft[:, 0:T], STAT[:, 0:T], AF.Sqrt, bias=cb_eps, scale=1.0 / dim)
            S.recip(RSTD[:, 0:T], ft[:, 0:T])

        def prenorm(gname, segs):
            T = sum(n for _, n in segs)
            for c in range(KC):
                sq = SQ[nxt("sq", 3)]
                S.act(sq[:, 0:T], hc(c)[:, 0:T], AF.Square)
                stats_mm(sq, segs, c == 0, c == KC - 1)
            finish_rstd(T, D)
            for c in range(KC):
                S.stt("dve", xc(c)[:, 0:T], hc(c)[:, 0:T], vec(gname, c), RSTD[:, 0:T], MUL, MUL)

        def prenorm_noscale(gname, segs):
            T = sum(n for _, n in segs)
            for c in range(KC):
                sq = SQ[nxt("sq", 3)]
                S.act(sq[:, 0:T], hc(c)[:, 0:T], AF.Square)
                stats_mm(sq, segs, c == 0, c == KC - 1)
                S.act(xc(c)[:, 0:T], hc(c)[:, 0:T], AF.Copy, scale=vec(gname, c))
            finish_rstd(T, D)

        class PostNorm:
            def __init__(self, ybuf, gname, scale, segs):
                self.y, self.g, self.scale, self.segs = ybuf, gname, scale, segs
                self.T = sum(n for _, n in segs)
                self.pend = []
                self.cnt = 0

            def add(self, m, src):
                T = self.T
                sq = SQ[nxt("sq", 3)]
                S.cp("dve", self.y(m)[:, 0:T], src)
                S.act(sq[:, 0:T], self.y(m)[:, 0:T], AF.Square)
                self.pend.append(sq)
                while len(self.pend) > 1:
                    self._flush()

            def _flush(self):
                sq = self.pend.pop(0)
                stats_mm(sq, self.segs, self.cnt == 0, self.cnt == KC - 1)
                self.cnt += 1

            def finish(self):
                T = self.T
                while self.pend:
                    self._flush()
                finish_rstd(T, D)
                for m in range(KC):
                    ft = FT[nxt("ft", 2)]
                    S.stt("dve", ft[:, 0:T], self.y(m)[:, 0:T], vec(self.g, m), RSTD[:, 0:T], MUL, MUL)
                    S.stt("dve", hc(m)[:, 0:T], ft[:, 0:T], float(self.scale), hc(m)[:, 0:T], MUL, ADD)

        pre_x = {"jobs": []}

        def x_jobs(src, blocks):
            return [(src, r0, n, c0, half) for (r0, n, c0) in blocks for half in range(2)]

        def x_slot(i):
            return SCR[:, (i % 4) * 1024:(i % 4 + 1) * 1024]

        def prefetch_x(src, blocks, k=2):
            jobs = x_jobs(src, blocks)[:k]
            for i, (sr, r0, n, c0, half) in enumerate(jobs):
                S.dma("sp", x_slot(i)[0:n, :], sr[r0:r0 + n, half * 1024:(half + 1) * 1024])
            pre_x["jobs"] = jobs

        def load_x(jobs):
            npre = len(pre_x["jobs"])
            assert jobs[:npre] == pre_x["jobs"]
            pre_x["jobs"] = []
            def issue(i):
                sr, r0, n, c0, half = jobs[i]
                S.dma("sp", x_slot(i)[0:n, :], sr[r0:r0 + n, half * 1024:(half + 1) * 1024])
            for i in range(npre, min(4, len(jobs))):
                issue(i)
            for i, (sr, r0, n, c0, half) in enumerate(jobs):
                slot = x_slot(i)
                if i >= 1 and i + 3 < len(jobs) and i + 3 >= 4:
                    issue(i + 3)
                for q in range(2):
                    pb = bank(nxt("mb", 6))
                    for j in range(4):
                        S.tr(pb[:, j * 128:j * 128 + n], slot[0:n, (q * 4 + j) * 128:(q * 4 + j + 1) * 128],
                             identf[0:n, 0:n])
                    c_lo = half * 8 + q * 4
                    o = hT[:, c_lo * TB:(c_lo + 4) * TB].rearrange("p (c t) -> p c t", t=TB)[:, :, c0:c0 + n]
                    i_ = pb.rearrange("p (i t) -> p i t", t=128)[:, :, 0:n]
                    S.cp(ev_eng(), o, i_)

        def store_y(dst, blocks):
            for (r0, n, c0) in blocks:
                for half in range(2):
                    slot = SCR[:, 2048 + state["xs"] * 1024:2048 + state["xs"] * 1024 + 1024]
                    nxt("xs", 2)
                    for q in range(2):
                        pb = bank(nxt("mb", 6))
                        for i in range(4):
                            c = half * 8 + q * 4 + i
                            S.tr(pb[0:n, i * 128:(i + 1) * 128], hc(c)[:, c0:c0 + n], identf)
                        S.cp(ev_eng(), slot[0:n, q * 512:(q + 1) * 512], pb[0:n, :])
                    S.dma("sp", dst[r0:r0 + n, half * 1024:(half + 1) * 1024], slot[0:n, :])

        def ffn(pfx, segs):
            T = sum(n for _, n in segs)
            prenorm_noscale(f"g_{pfx}_pre", segs)
            milestone("ffn prenorm")
            wg, wu, wd = W[f"w_{pfx}_gate"], W[f"w_{pfx}_up"], W[f"w_{pfx}_down"]
            for jp in range(FC // 2):
                if jp == 1:
                    milestone("ffn first pair")
                gw = fm_group(wg, D, 256 * jp, 256, xc, segs)
                sts = []
                for i, w in enumerate(gw):
                    ft = FT[nxt("ft", 2)]
                    S.tt("dve", ft[:, 0:T], w, RSTD[:, 0:T], MUL)
                    stm = STM[nxt("stm", 4)]
                    S.act(stm[:, 0:T], ft[:, 0:T], AF.Silu)
                    S.tt("dve", stm[:, 0:T], stm[:, 0:T], RSTD[:, 0:T], MUL)
                    sts.append(stm)
                uw = fm_group(wu, D, 256 * jp, 256, xc, segs)
                for i, w in enumerate(uw):
                    S.tt("dve", gc(2 * jp + i)[:, 0:T], w, sts[i][:, 0:T], MUL)
            milestone("ffn gate/up")
            pn = PostNorm(xc, f"g_{pfx}_post", 0.5, segs)
            for mp in range(8):
                wins = fm_group(wd, DFF, 256 * mp, 256, gc, segs)
                for i, w in enumerate(wins):
                    pn.add(2 * mp + i, w)
            milestone("ffn down")
            pn.finish()

        GT = lambda i: SCR[0:4, i * TB:(i + 1) * TB]

        def gates(segs, nblk, with_s):
            T = sum(n for _, n in segs)
            w_in = W["w_in"]
            wins = []
            pan = load_panel(w_in, 0, 16, O_I, 8)
            wbase = [1024 * nxt("win", 3) for _ in range(2)]
            for gi in range(2):
                for kc in range(KC):
                    for (c0, n) in segs:
                        S.mm(PS[0:4, wbase[gi] + c0:wbase[gi] + c0 + n], pan[:, kc, gi * 4:(gi + 1) * 4],
                             xc(kc)[:, c0:c0 + n], start=(kc == 0), stop=(kc == KC - 1))
            ips = PS[0:4, wbase[0]:wbase[0] + T]
            fps = PS[0:4, wbase[1]:wbase[1] + T]
            iT, eT, lT, LT, aT, cT, dT = [GT(i)[:, 0:T] for i in range(7)]
            S.ts("dve", iT, ips, BIF[:, 0:1], None, ADD)
            S.act(eT, fps, AF.Exp, bias=BIF[:, 2:3], scale=-1.0)
            S.act(lT, eT, AF.Ln, bias=cb_one[0:4, :], scale=1.0)
            S.scan(LT, rst[:, 0:T], lT, 0.0, MUL, ADD)
            S.tt("dve", aT, iT, LT, ADD)
            A = GSM[:, 0:nblk]
            nL = GSM[:, 4:4 + nblk]
            Mv = GSM[:, 8:8 + nblk]
            cv = GSM[:, 12:12 + nblk]
            MP = GSM[:, 16:16 + nblk]
            dC = GSM[:, 20:20 + nblk]
            S.rmax(A, aT[:, 0:128 * nblk].rearrange("p (b t) -> p b t", t=128))
            S.ts("dve", nL, LT[:, 127:128 * nblk:128], -1.0, None, MUL)
            S.scan(Mv, A, nL, Mst[:, 0:1], MAX, ADD)
            S.tt("dve", cv, Mv, nL, SUB)
            S.cp("dve", MP[:, 0:1], Mst[:, 0:1])
            if nblk > 1:
                S.cp("dve", MP[:, 1:nblk], Mv[:, 0:nblk - 1])
            S.tt("dve", dC, MP, cv, SUB)
            S.act(dC, dC, AF.Exp)
            S.cp("dve", Mst[:, 0:1], Mv[:, nblk - 1:nblk])
            S.cp("dve", cT[:, 0:128 * nblk].rearrange("p (b t) -> p b t", t=128),
                 cv.unsqueeze(2).to_broadcast([4, nblk, 128]))
            if with_s:
                As = GSM[:, 24:40]
                cs = GSM[:, 40:56]
                m0T = SCR[0:4, 7 * TB:7 * TB + 16]
                S.dma("sp", m0T, msd.rearrange("j h -> h j"), allow_slow_non_contiguous=True)
                S.rmax(As, aT[:, 512:576].rearrange("p (j r) -> p j r", r=4))
                S.tt("dve", cs, m0T, As, MAX)
                mnew = SCR[0:4, 7 * TB + 16:7 * TB + 32]
                dCs = SCR[0:4, 7 * TB + 32:7 * TB + 48]
                S.tt("dve", dCs, m0T, cs, SUB)
                S.act(dCs, dCs, AF.Exp)
                S.cp("dve", cT[:, 512:576].rearrange("p (j r) -> p j r", r=4), cs.unsqueeze(2).to_broadcast([4, 16, 4]))
            S.tt("dve", dT, aT, cT, SUB)
            S.act(dT, dT, AF.Exp, bias=cb_nl16[0:4, :], scale=1.0)
            S.tt("dve", eT, LT, cT, SUB)
            S.act(THR[:, 0:T], eT, AF.Exp)
            blocks = [(128 * b, 128) for b in range(nblk)] + ([(512, 64)] if with_s else [])
            pb = bank(nxt("mb", 6))
            for bi, (c0, n) in enumerate(blocks):
                S.tr(pb[0:n, bi * 4:bi * 4 + 4], dT[:, c0:c0 + n], identf[0:4, 0:4])
                S.cp("dve", WKT[0:n, bi * 4:bi * 4 + 4], pb[0:n, bi * 4:bi * 4 + 4])
            pb2 = bank(nxt("mb", 6))
            for h in range(4):
                S.mm(pb2[:, h * 20:h * 20 + nblk], sel(h), dC)
                if with_s:
                    S.mm(pb2[:, h * 20 + 4:h * 20 + 20], sel(h), dCs)
            S.cp("dve", DCB[:], pb2[:, 0:80])
            if with_s:
                S.tt("dve", mnew, cs, LT[:, 512 + 3:576:4], SUB)
                S.dma("sp", m_s.rearrange("j h -> h j"), mnew, allow_slow_non_contiguous=True)

        def conv_chunk(c, win, segs, out, do_conv=True):
            T = sum(n for _, n in segs)
            with_s = len(segs) > 1
            xp = XP[0]
            np_ = segs[0][1]
            S.cp("act", xp[:, 3:3 + np_], win[:, 0:np_])
            S.cp("dve", xp[:, 0:3], HP[:, c * 3:c * 3 + 3])
            S.cp("dve", HP[:, c * 3:c * 3 + 3], xp[:, np_:np_ + 3])
            if with_s:
                xps = xp[:, 520:632].rearrange("p (j r) -> p j r", r=7)
                S.cp("act", xps[:, :, 3:7], win[:, 512:576].rearrange("p (j r) -> p j r", r=4))
                S.cp("dve", xps[:, :, 0:3], HS[:, c * 48:(c + 1) * 48].rearrange("p (j r) -> p j r", r=3))
                S.cp("dve", HS[:, c * 48:(c + 1) * 48].rearrange("p (j r) -> p j r", r=3), xps[:, :, 4:7])
            if not do_conv:
                return
            acc = FT[nxt("ft", 2)]
            S.ts("dve", acc[:, 0:np_], xp[:, 0:np_], vec("w_conv0", c), vec("b_conv", c), MUL, ADD)
            for j in range(1, 4):
                S.stt("dve", acc[:, 0:np_], xp[:, j:j + np_], vec(f"w_conv{j}", c), acc[:, 0:np_], MUL, ADD)
            if with_s:
                accs = acc[:, 512:576].rearrange("p (j r) -> p j r", r=4)
                S.ts("dve", accs, xps[:, :, 0:4], vec("w_conv0", c), vec("b_conv", c), MUL, ADD)
                for j in range(1, 4):
                    S.stt("dve", accs, xps[:, :, j:j + 4], vec(f"w_conv{j}", c), accs, MUL, ADD)
            S.act(out[:, 0:T], acc[:, 0:T], AF.Silu)

        def haT(c):
            return G[:, c * TB:(c + 1) * TB]

        def mixin(c):
            return G[:, 9216 + c * TB:9216 + (c + 1) * TB]

        def qkT(i):
            return G[:, 18432 + i * TB:18432 + (i + 1) * TB]

        def vtok(b):
            return G[:, 20736 + b * 512:20736 + (b + 1) * 512]

        def ktok(b):
            return G[:, 23296 + b * 256:23296 + (b + 1) * 256]

        def Ch(h):
            return Cst[:, h * 1024:(h + 1) * 1024]

        def head_proj(h, segs, blocks, full):
            w_in = W["w_in"]
            if full is True:
                qw = fm_group(w_in, D, O_QK + 256 * h, 256, xc, segs)
                for i, w in enumerate(qw):
                    conv_chunk(2 * h + i, w, segs, qkT(i), do_conv=True)
            elif full == "hist":
                qw = fm_group(w_in, D, O_QK + 256 * h, 256, lambda kc: xc(kc)[:, 384:512], [(0, 128)])
                for i, w in enumerate(qw):
                    conv_chunk(2 * h + i, w, [(0, 128)], None, do_conv=False)
            kw_ = fm_group(w_in, D, O_QK + 1024 + 256 * h, 256, xc, segs)
            for i, w in enumerate(kw_):
                conv_chunk(8 + 2 * h + i, w, segs, qkT(2 + i))

            def cons_v(bi, n, ps):
                S.cp(ev_eng(), vtok(bi)[0:n, :], ps)
            tm_group(w_in, O_VA + 512 * h, blocks, cons_v)
            for bi, (c0, n) in enumerate(blocks):
                pk = bank(nxt("mb", 6))
                for dc in range(2):
                    S.mm(pk[0:n, dc * 128:(dc + 1) * 128], qkT(2 + dc)[:, c0:c0 + n], identb[:])
                S.ts("dve", ktok(bi)[0:n, :], pk[0:n, 0:256], WKT[0:n, bi * 4 + h:bi * 4 + h + 1], None, MUL)

        def state_update(h, bi, n, dcol, src_k=None, alt=0):
            pU = PS[:, 1024:2048] if alt == 0 else PS[:, 2048:3072]
            kt = ktok(bi) if src_k is None else src_k
            for dc in range(2):
                S.mm(pU[:, dc * 512:(dc + 1) * 512], kt[0:n, dc * 128:(dc + 1) * 128], vtok(bi)[0:n, :])
            pn_ = PS[:, 3648:3650]
            for dc in range(2):
                S.mm(pn_[:, dc:dc + 1], kt[0:n, dc * 128:(dc + 1) * 128], onesb[0:n, 0:1])
            S.stt("dve", Ch(h), Ch(h), dcol, pU, MUL, ADD)
            S.stt("dve", Nst[:, 2 * h:2 * h + 2], Nst[:, 2 * h:2 * h + 2], dcol, pn_, MUL, ADD)

        def mlstm_out(h, c0, n, small, numps, denps, thr_ps):
            dabs, mx, e2 = MLS[:, 0:n], MLS[:, 128:128 + n], MLS[:, 256:256 + n]
            S.act(dabs, denps, AF.Abs)
            S.tt("dve", mx, dabs, thr_ps, MAX)
            S.stt("dve", e2, mx, EPS, mx, MUL, MUL)
            sqv = SQV[:, 0:4 * n].rearrange("p (v t) -> p v t", t=n)
            S.act(sqv, numps, AF.Square)
            pSS = small[:, 384:384 + n]
            for vc in range(4):
                S.mm(pSS, onesb[:], sqv[:, vc, :], sg=True, start=False, stop=(vc == 3))
            S.stt("dve", dabs, pSS, 1.0 / 512, e2, MUL, ADD)
            S.act(mx, dabs, AF.Sqrt)
            S.recip(e2, mx)
            for vc in range(4):
                S.stt("dve", haT(4 * h + vc)[:, c0:c0 + n], numps[:, vc, :], vec("g_head", 4 * h + vc), e2, MUL, MUL)

        def blk_banks(bi):
            if bi % 2 == 0:
                return bank(0), bank(1), PS[:, 1024:2048]
            return bank(4), bank(5), PS[:, 3072:4096]

        def mlstm_blockA(h, bi, c0):
            n = 128
            small, pN, pU = blk_banks(bi)
            pS = small[:, 0:128]
            for dc in range(2):
                S.mm(pS, qkT(2 + dc)[:, c0:c0 + n], qkT(dc)[:, c0:c0 + n], sg=True, start=(dc == 0), stop=(dc == 1))
            pt = PTb[:, (bi % 2) * 128:(bi % 2) * 128 + 128]
            S.stt("dve", pt, pS, WKT[:, bi * 4 + h:bi * 4 + h + 1], trilT, MUL, MUL)
            S.mm(small[:, 128:256], onesb[:], pt, sg=True, start=True, stop=False)
            S.mm(small[:, 256:384], sel(h), THR[:, c0:c0 + n], sg=True, start=False, stop=True)
            for vc in range(4):
                S.mm(pN[:, vc * 128:(vc + 1) * 128], vtok(bi)[:, vc * 128:(vc + 1) * 128], pt, sg=True, start=(vc == 0), stop=False)
            for dc in range(2):
                S.mm(pU[:, dc * 512:(dc + 1) * 512], ktok(bi)[:, dc * 128:(dc + 1) * 128], vtok(bi)[:, :])
            for dc in range(2):
                S.mm(small[:, dc:dc + 1], ktok(bi)[:, dc * 128:(dc + 1) * 128], onesb[:, 0:1], sg=True, start=False, stop=True)

        def mlstm_blockB(h, bi, c0):
            n = 128
            small, pN, pU = blk_banks(bi)
            dcol = DCB[:, h * 20 + bi:h * 20 + bi + 1]
            cbi = nxt("cb", 2)
            cb = Cb[:, cbi * 1024:(cbi + 1) * 1024]
            S.act(cb, Ch(h), AF.Copy, scale=dcol)
            nrep = NREP[:, cbi * 256:(cbi + 1) * 256]
            for dc in range(2):
                S.ts("dve", nrep[:, dc * 128:(dc + 1) * 128], Nst[:, 2 * h + dc:2 * h + dc + 1].to_broadcast([128, 128]),
                     dcol, None, MUL)
            pD = small[:, 128:256]
            for dc in range(2):
                S.mm(pD, nrep[:, dc * 128:(dc + 1) * 128], qkT(dc)[:, c0:c0 + n], sg=True, start=False, stop=(dc == 1))
            for vc in range(4):
                for dc in range(2):
                    S.mm(pN[:, vc * 128:(vc + 1) * 128], cb[:, dc * 512 + vc * 128:dc * 512 + (vc + 1) * 128],
                         qkT(dc)[:, c0:c0 + n], sg=True, start=False, stop=(dc == 1 and vc == 3))
            S.stt("dve", Ch(h), Ch(h), dcol, pU, MUL, ADD)
            S.stt("dve", Nst[:, 2 * h:2 * h + 2], Nst[:, 2 * h:2 * h + 2], dcol, small[:, 0:2], MUL, ADD)
            mlstm_out(h, c0, n, small, pN.rearrange("p (v t) -> p v t", t=128), pD, small[:, 256:384])

        def mlstm_prompt(h, nblk):
            mlstm_blockA(h, 0, 0)
            for bi in range(nblk):
                if bi + 1 < nblk:
                    mlstm_blockA(h, bi + 1, 128 * (bi + 1))
                mlstm_blockB(h, bi, 128 * bi)

        def mlstm_samples(h):
            c0, n = 512, 64
            bi = 4
            small = bank(0)
            pS = small[0:64, 0:64]
            for dc in range(2):
                S.mm(pS, qkT(2 + dc)[:, c0:c0 + n], qkT(dc)[:, c0:c0 + n], start=(dc == 0), stop=(dc == 1))
            pt = PTb[0:64, 0:64]
            S.stt("dve", pt, pS, WKT[0:64, bi * 4 + h:bi * 4 + h + 1], maskS, MUL, MUL)
            dcs = DCB[:, h * 20 + 4:h * 20 + 20]
            n0s = NNT[:, (2 * h) * 16:(2 * h + 2) * 16].rearrange("p (d j) -> p d j", j=16)
            S.tt("dve", n0s, N0T[:, (2 * h) * 16:(2 * h + 2) * 16].rearrange("p (d j) -> p d j", j=16),
                 dcs.unsqueeze(1).to_broadcast([128, 2, 16]), MUL)
            pD = small[:, 128:128 + n]
            S.mm(pD, onesb[0:64, :], pt, start=True, stop=False)
            for dc in range(2):
                qn = QN[:, dc * 64:(dc + 1) * 64]
                S.tt("dve", qn.rearrange("p (j r) -> p j r", r=4), qkT(dc)[:, c0:c0 + n].rearrange("p (j r) -> p j r", r=4),
                     n0s[:, dc, :].unsqueeze(2).to_broadcast([128, 16, 4]), MUL)
                S.mm(pD, onesb[:], qn, start=False, stop=(dc == 1))
            pN = bank(1)
            for vc in range(4):
                S.mm(pN[:, vc * 64:(vc + 1) * 64], vtok(bi)[0:64, vc * 128:(vc + 1) * 128], pt, start=(vc == 0), stop=False)
            def c0s(j):
                return SCR[:, (j % 4) * 1024:(j % 4 + 1) * 1024]

            def issue_in(j):
                S.dma("pool", c0s(j).rearrange("p (d v) -> p d v", v=512), Cs[j, h].rearrange("(d p) v -> p d v", p=128))
            for j in range(3):
                issue_in(j)
            for j in range(16):
                if j + 3 < 16:
                    issue_in(j + 3)
                c0slot = c0s(j)
                dcol = DCB[:, h * 20 + 4 + j:h * 20 + 5 + j]
                cbi = nxt("cb", 2)
                cb = Cb[:, cbi * 1024:(cbi + 1) * 1024]
                S.act(cb, c0slot, AF.Copy, scale=dcol)
                for vc in range(4):
                    for dc in range(2):
                        S.mm(pN[:, vc * 64 + 4 * j:vc * 64 + 4 * j + 4], cb[:, dc * 512 + vc * 128:dc * 512 + (vc + 1) * 128],
                             qkT(dc)[:, c0 + 4 * j:c0 + 4 * j + 4], start=False, stop=(dc == 1 and vc == 3 and j == 15))
                km = KM[:, (j % 2) * 256:(j % 2) * 256 + 256]
                S.ts("dve", km, ktok(bi)[0:64, :], seqmask[:, j:j + 1], None, MUL)
                pU = PS[:, 1024:2048] if j % 2 == 0 else PS[:, 2048:3072]
                for dc in range(2):
                    S.mm(pU[:, dc * 512:(dc + 1) * 512], km[:, dc * 128:(dc + 1) * 128], vtok(bi)[0:64, :])
                S.stt("dve", c0slot, c0slot, dcol, pU, MUL, ADD)
                S.dma("sp", C_s[j, h].rearrange("(d p) v -> p d v", p=128), c0slot.rearrange("p (d v) -> p d v", v=512))
            thr_ps = small[:, 256:256 + n]
            S.mm(thr_ps, sel(h), THR[:, c0:c0 + n])
            mlstm_out(h, c0, n, small, pN[:, 0:256].rearrange("p (v t) -> p v t", t=64), pD, thr_ps)
            pnn = PS[:, 3648:3648 + 32]
            for dc in range(2):
                S.mm(pnn[:, dc * 16:(dc + 1) * 16], ktok(bi)[0:64, dc * 128:(dc + 1) * 128], seqmb[:])
            S.tt("dve", NNT[:, (2 * h) * 16:(2 * h + 2) * 16], NNT[:, (2 * h) * 16:(2 * h + 2) * 16], pnn, ADD)

        def mixer_state_only(segs, nblk, last):
            blocks = [(128 * b, 128) for b in range(nblk)]
            prenorm("g_mix_pre", segs)
            gates(segs, nblk, False)
            for h in range(4):
                head_proj(h, segs, blocks, full=("hist" if last else "skip"))
                for bi in range(nblk):
                    state_update(h, bi, 128, DCB[:, h * 20 + bi:h * 20 + bi + 1], alt=bi % 2)

        def mixer(segs, nblk, with_s):
            T = sum(n for _, n in segs)
            blocks = [(128 * b, 128) for b in range(nblk)] + ([(512, 64)] if with_s else [])
            w_in = W["w_in"]
            prenorm("g_mix_pre", segs)
            uT = haT

            def LNx(b):
                return G[:, 9216 + b * 2048:9216 + (b + 1) * 2048]
            for cp_ in range(8):
                wins = fm_group(w_in, D, O_UB + 256 * cp_, 256, xc, segs)
                for i, w in enumerate(wins):
                    S.act(uT(2 * cp_ + i)[:, 0:T], w, AF.Gelu)
            VS32 = SCR[0:64, 0:2048]
            S.memset("dve", LNS[:], 0.0)
            for g in range(4):
                def cons_vb(bi, n, ps, g=g):
                    S.act(LNx(bi)[0:n, g * 512:(g + 1) * 512], ps, AF.Gelu, accum=LNS[0:n, bi * 16 + g:bi * 16 + g + 1])
                    junk = FT[nxt("ft", 2)]
                    if bi == 4:
                        S.act(VS32[:, g * 512:(g + 1) * 512], ps, AF.Gelu)
                    S.act(junk[0:n, 0:512], LNx(bi)[0:n, g * 512:(g + 1) * 512], AF.Square,
                          accum=LNS[0:n, bi * 16 + 4 + g:bi * 16 + 5 + g])
                tm_group(w_in, O_VB + 512 * g, blocks, cons_vb)
            for bi, (c0, n) in enumerate(blocks):
                L = lambda k: LNS[0:n, bi * 16 + k:bi * 16 + k + 1]
                S.rsum(L(8), LNS[0:n, bi * 16:bi * 16 + 4])
                S.rsum(L(9), LNS[0:n, bi * 16 + 4:bi * 16 + 8])
                S.ts("dve", L(10), L(8), 1.0 / D, None, MUL)
                S.tt("dve", L(11), L(10), L(10), MUL)
                S.stt("dve", L(12), L(9), 1.0 / D, L(11), MUL, SUB)
                S.act(L(13), L(12), AF.Sqrt, bias=cb_eps[0:n, :], scale=1.0)
                S.recip(L(14), L(13))
                S.stt("dve", L(15), L(10), -1.0, L(14), MUL, MUL)
                if bi == 4:
                    for g in range(4):
                        grep = SCR[0:64, 2048:2560]
                        brep = SCR[0:64, 2560:3072]
                        tmp = SCR[0:64, 3072 + (g % 2) * 512:3584 + (g % 2) * 512]
                        S.dma("sp", grep, vrow[VI["g_ln_v"]:VI["g_ln_v"] + 1, g * 512:(g + 1) * 512].to_broadcast([64, 512]))
                        S.dma("sp", brep, vrow[VI["b_ln_v"]:VI["b_ln_v"] + 1, g * 512:(g + 1) * 512].to_broadcast([64, 512]))
                        S.ts("dve", tmp, VS32[:, g * 512:(g + 1) * 512], L(14), L(15), MUL, ADD)
                        S.tt("dve", tmp, tmp, grep, MUL)
                        S.tt("dve", tmp, tmp, brep, ADD)
                        S.dma("sp", v_s[:, g * 512:(g + 1) * 512], tmp)
                S.ts("dve", LNx(bi)[0:n, :], LNx(bi)[0:n, :], L(14), L(15), MUL, ADD)
            B2 = lambda c: SCR[:, c * 128:(c + 1) * 128]
            B2s = lambda c: SCR[:, 2048 + c * 64:2048 + (c + 1) * 64]
            bsb = SCR[:, 3072:3584]
            bsbs = SCR[:, 3584:3840]
            pw = bank(nxt("mb", 6))
            pws = bank(nxt("mb", 6))
            for g in range(4):
                S.dma("sp", bsb[:, g * 128:(g + 1) * 128], b_sp[g:g + 1, :].to_broadcast([128, 128]))
                S.mm(pw[:, g * 128:(g + 1) * 128], onesb[:], WgT[:, g * 128:(g + 1) * 128])
                if with_s:
                    S.dma("sp", bsbs[:, g * 64:(g + 1) * 64].rearrange("p (j r) -> p j r", r=4),
                          b_sp[g:g + 1, 0:4].unsqueeze(1).to_broadcast([128, 16, 4]))
                    S.mm(pws[:, g * 64:(g + 1) * 64], onesb[0:64, :], WsT[:, g * 64:(g + 1) * 64])
            for c in range(KC):
                g = c // 4
                S.stt("dve", B2(c), pw[:, g * 128:(g + 1) * 128], vec("b_ln_v", c), bsb[:, g * 128:(g + 1) * 128], MUL, ADD)
                if with_s:
                    S.stt("dve", B2s(c), pws[:, g * 64:(g + 1) * 64], vec("b_ln_v", c), bsbs[:, g * 64:(g + 1) * 64], MUL, ADD)
            for c in range(KC):
                g = c // 4
                wb = 1024 * nxt("win", 3)
                for b in range(nblk):
                    S.mm(PS[:, wb + b * 128:wb + (b + 1) * 128], LNx(b)[:, c * 128:(c + 1) * 128], WgT[:, g * 128:(g + 1) * 128])
                if with_s:
                    S.mm(PS[:, wb + 512:wb + 576], LNx(4)[0:64, c * 128:(c + 1) * 128], WsT[:, g * 64:(g + 1) * 64])
                ft = FT[nxt("ft", 2)]
                S.stt("dve", ft[:, 0:128 * nblk].rearrange("p (b t) -> p b t", t=128),
                      PS[:, wb:wb + 128 * nblk].rearrange("p (b t) -> p b t", t=128), vec("g_ln_v", c),
                      B2(c).unsqueeze(1).to_broadcast([128, nblk, 128]), MUL, ADD)
                if with_s:
                    S.stt("dve", ft[:, 512:576], PS[:, wb + 512:wb + 576], vec("g_ln_v", c), B2s(c), MUL, ADD)
                S.tt("dve", uT(c)[:, 0:T], uT(c)[:, 0:T], ft[:, 0:T], MUL)
            for cp_ in range(8):
                gw = fm_group(w_in, D, O_GB + 256 * cp_, 256, xc, segs)
                sts = []
                for i, w in enumerate(gw):
                    stm = STM[nxt("stm", 4)]
                    S.act(stm[:, 0:T], w, AF.Sigmoid)
                    sts.append(stm)
                yw = fm_group(W["w_b_out"], D, 256 * cp_, 256, uT, segs)
                for i, w in enumerate(yw):
                    S.tt("dve", mixin(2 * cp_ + i)[:, 0:T], w, sts[i][:, 0:T], MUL)
            if with_s:
                nrow = SCR[0:16, 2048:3072]
                S.dma("sp", nrow, nsd)
                pb = bank(nxt("mb", 6))
                for k in range(8):
                    S.tr(pb[:, k * 16:(k + 1) * 16], nrow[:, k * 128:(k + 1) * 128], identf[0:16, 0:16])
                S.cp("dve", N0T[:], pb[:, 0:128])
            gates(segs, nblk, with_s)
            for h in range(4):
                head_proj(h, segs, blocks, full=True)
                mlstm_prompt(h, nblk)
                if with_s:
                    mlstm_samples(h)
            if with_s:
                pb = bank(nxt("mb", 6))
                for k in range(4):
                    S.tr(pb[0:16, k * 128:(k + 1) * 128], NNT[:, k * 16:(k + 1) * 16], identf)
                nrow = SCR[0:16, 2048:3072]
                S.cp("dve", nrow[:, 0:512], pb[0:16, :])
                pb2 = bank(nxt("mb", 6))
                for k in range(4):
                    S.tr(pb2[0:16, k * 128:(k + 1) * 128], NNT[:, (4 + k) * 16:(5 + k) * 16], identf)
                S.cp("dve", nrow[:, 512:1024], pb2[0:16, :])
                S.dma("sp", n_s, nrow)
            for cp_ in range(8):
                wins = fm_group(w_in, D, O_OA + 256 * cp_, 256, xc, segs)
                for i, w in enumerate(wins):
                    stm = STM[nxt("stm", 4)]
                    S.act(stm[:, 0:T], w, AF.Sigmoid)
                    S.tt("dve", haT(2 * cp_ + i)[:, 0:T], haT(2 * cp_ + i)[:, 0:T], stm[:, 0:T], MUL)
            for cp_ in range(8):
                gw = fm_group(w_in, D, O_GA + 256 * cp_, 256, xc, segs)
                sts = []
                for i, w in enumerate(gw):
                    stm = STM[nxt("stm", 4)]
                    S.act(stm[:, 0:T], w, AF.Sigmoid)
                    sts.append(stm)
                yw = fm_group(W["w_a_out"], D, 256 * cp_, 256, haT, segs)
                for i, w in enumerate(yw):
                    ft = FT[nxt("ft", 2)]
                    S.tt("dve", ft[:, 0:T], w, sts[i][:, 0:T], MUL)
                    S.tt("dve", mixin(2 * cp_ + i)[:, 0:T], mixin(2 * cp_ + i)[:, 0:T], ft[:, 0:T], ADD)
            pn = PostNorm(xc, "g_mix_post", 1.0, segs)
            for mp in range(8):
                wins = fm_group(W["w_o"], D, 256 * mp, 256, mixin, segs)
                for i, w in enumerate(wins):
                    pn.add(2 * mp + i, w)
            pn.finish()

        def ple(segs, pblocks):
            T = sum(n for _, n in segs)
            prenorm_noscale("g_ple_pre", segs)
            pT = lambda c: G[:, c * TB:(c + 1) * TB]
            for (src, r0, n, c0) in pblocks:
                slot = SCR[:, state["xs"] * 1024:state["xs"] * 1024 + 256]
                nxt("xs", 2)
                S.dma("sp", slot[0:n, :], src[r0:r0 + n, :])
                pb = bank(nxt("mb", 6))
                for i in range(2):
                    S.tr(pb[:, i * 128:i * 128 + n], slot[0:n, i * 128:(i + 1) * 128], identf[0:n, 0:n])
                    S.cp(ev_eng(), pT(i)[:, c0:c0 + n], pb[:, i * 128:i * 128 + n])
            yb = lambda m: G[:, 9216 + m * TB:9216 + (m + 1) * TB]
            pn = PostNorm(yb, "g_ple_post", 1.0, segs)
            for cp_ in range(8):
                gw = fm_group(W["w_ple_gate"], D, 256 * cp_, 256, xc, segs)
                sts = []
                for i, w in enumerate(gw):
                    ft = FT[nxt("ft", 2)]
                    S.tt("dve", ft[:, 0:T], w, RSTD[:, 0:T], MUL)
                    stm = STM[nxt("stm", 4)]
                    S.act(stm[:, 0:T], ft[:, 0:T], AF.Sigmoid)
                    sts.append(stm)
                uw = fm_group(W["w_ple_up"], 256, 256 * cp_, 256, pT, segs)
                for i, w in enumerate(uw):
                    ft = FT[nxt("ft", 2)]
                    S.tt("dve", ft[:, 0:T], w, sts[i][:, 0:T], MUL)
                    pn.add(2 * cp_ + i, ft[:, 0:T])
            pn.finish()

        ms = {"n": 0}

        def milestone(name):
            ms["n"] += 1
            S.tag = f"{ms['n']:02d} {name}"
            if stop_at is not None and ms["n"] >= stop_at:
                print("STOP at milestone", ms["n"], name, flush=True)
                raise _Stop()

        def _program():
            seg_p = [(0, 512)]
            seg_ps = [(0, 512), (512, 64)]
            milestone("setup")
            def pre_blocks(t):
                return [(t * 512 + 128 * b, 128, 128 * b) for b in range(4)]

            def main_jobs(t):
                j = x_jobs(xmain, [(t * 512 + 128 * b, 128, 128 * b) for b in range(4)])
                if t == 1:
                    j = j + x_jobs(xs, [(0, 64, 512)])
                return j
            for t in range(2):
                load_x(x_jobs(xpre, pre_blocks(t)))
                milestone(f"p{t} load_x")
                ffn("ffn1", seg_p)
                milestone(f"p{t} ffn1")
                mixer_state_only(seg_p, 4, last=(t == 1))
                milestone(f"p{t} mixer_state")
                if t == 0:
                    prefetch_x(xpre, pre_blocks(1))
                else:
                    prefetch_x(xmain, [(128 * b, 128, 128 * b) for b in range(4)])
            S.ts("dve", Cst[:], Cst[:], MKC[:, 0:1], None, MUL)
            S.ts("dve", Nst[:], Nst[:], MKC[:, 0:1], None, MUL)
            S.ts("dve", Mst[:], Mst[:], MKC[0:4, 0:1], None, MUL)
            crow = SCR[0:48, 2048:4096]
            S.dma("sp", crow, convs)
            for q in range(4):
                pb = bank(nxt("mb", 6))
                for i in range(4):
                    S.tr(pb[:, i * 48:(i + 1) * 48], crow[:, (q * 4 + i) * 128:(q * 4 + i + 1) * 128], identf[0:48, 0:48])
                S.cp("dve", HS[:, q * 192:(q + 1) * 192], pb[:, 0:192])
            for t in range(2):
                with_s = (t == 1)
                segs = seg_ps if with_s else seg_p
                xb = [(t * 512 + 128 * b, 128, 128 * b) for b in range(4)]
                load_x(main_jobs(t))
                milestone(f"m{t} load_x")
                ffn("ffn1", segs)
                milestone(f"m{t} ffn1")
                mixer(segs, 4, with_s)
                milestone(f"m{t} mixer")
                ffn("ffn2", segs)
                milestone(f"m{t} ffn2")
                pbl = [(pmain, t * 512 + 128 * b, 128, 128 * b) for b in range(4)] + ([(pss, 0, 64, 512)] if with_s else [])
                ple(segs, pbl)
                milestone(f"m{t} ple")
                if t == 0:
                    prefetch_x(xmain, [(512 + 128 * b, 128, 128 * b) for b in range(4)])
                store_y(y_main, xb)
                if with_s:
                    store_y(y_s, [(0, 64, 512)])
                milestone(f"m{t} store")
            for h in range(4):
                S.dma("sp", C_p[h].rearrange("(d p) v -> p d v", p=128), Ch(h).rearrange("p (d v) -> p d v", v=512))
            pb = bank(nxt("mb", 6))
            S.tr(pb[0:8, 0:128], Nst[:, 0:8], identf)
            nrow = SCR[0:8, 0:128]
            S.cp("dve", nrow, pb[0:8, 0:128])
            S.dma("sp", n_p, nrow)
            S.dma("sp", m_p, Mst[:, 0:1])
            pb = bank(nxt("mb", 6))
            for c in range(KC):
                S.tr(pb[0:3, (c % 4) * 128:(c % 4 + 1) * 128], HP[:, c * 3:c * 3 + 3], identf)
                if c % 4 == 3:
                    S.cp("dve", SCR[0:3, 1024 + (c - 3) * 128:1024 + (c + 1) * 128], pb[0:3, :])
                    if c < KC - 1:
                        pb = bank(nxt("mb", 6))
            S.dma("sp", conv_p, SCR[0:3, 1024:1024 + 2048])
            pb = bank(nxt("mb", 6))
            for c in range(KC):
                S.tr(pb[0:48, (c % 4) * 128:(c % 4 + 1) * 128], HS[:, c * 48:(c + 1) * 48], identf)
                if c % 4 == 3:
                    S.cp("dve", SCR[0:48, 0 + (c - 3) * 128:(c + 1) * 128] if False else crow[:, (c - 3) * 128:(c + 1) * 128], pb[0:48, :])
                    if c < KC - 1:
                        pb = bank(nxt("mb", 6))
            S.dma("sp", conv_s, crow)

        try:
            _program()
        except _Stop:
            pass
        S.emit()
        print("program: ops", len(S.ops), "sems", S.nsems, flush=True)
    return nc


_CACHE = {}


def kernel(**inp):
    f32 = np.float32
    A = {k: np.asarray(v) for k, v in inp.items() if not k.startswith("_")}
    if inp.get("_return_in_maps"):
        _CACHE.setdefault("nc", None)
    if "nc" not in _CACHE:
        _CACHE["nc"] = build_program()
    nc = _CACHE["nc"]
    consts = make_consts()
    vecs = np.stack([A['g_ffn1_pre'][0], A['g_ffn1_post'][0], A['g_mix_pre'][0], A['b_conv'][0],
                     A['w_conv'][0, 0], A['w_conv'][0, 1], A['w_conv'][0, 2], A['w_conv'][0, 3],
                     A['g_head'][0], A['g_ln_v'][0], A['b_ln_v'][0], A['g_mix_post'][0], A['g_ffn2_pre'][0],
                     A['g_ffn2_post'][0], A['g_ple_pre'][0], A['g_ple_post'][0]]).astype(f32)
    bif = np.stack([A['b_igate'][0], A['b_fgate'][0]], axis=1).astype(f32)
    shared = {
        "vecs": np.ascontiguousarray(vecs.reshape(256, 128)), "vrow": vecs, "bif": bif, "consts": consts,
        "w_spatial": np.ascontiguousarray(A['w_spatial'][0]), "b_spatial": np.ascontiguousarray(A['b_spatial'][0]),
    }
    for nm in ["w_ffn1_gate", "w_ffn1_up", "w_ffn1_down", "w_in", "w_a_out", "w_b_out", "w_o",
               "w_ffn2_gate", "w_ffn2_up", "w_ffn2_down", "w_ple_gate", "w_ple_up"]:
        shared[nm] = np.ascontiguousarray(A[nm][0])
    in_maps = []
    xp_, pp_ = A['x_prompt'], A['p_prompt'][0]
    for c in range(8):
        s, hf = c // 2, c % 2
        m = dict(shared)
        m["xmain"] = np.ascontiguousarray(xp_[s, hf * 1024:(hf + 1) * 1024])
        m["xpre"] = np.ascontiguousarray(xp_[s, 0:1024]) if hf == 1 else np.zeros((1024, D), f32)
        m["pmain"] = np.ascontiguousarray(pp_[s, hf * 1024:(hf + 1) * 1024])
        m["xs"] = np.ascontiguousarray(A['x_sample'][16 * c:16 * c + 16].reshape(64, D))
        m["ps"] = np.ascontiguousarray(A['p_sample'][0, 16 * c:16 * c + 16].reshape(64, 256))
        m["convs"] = np.ascontiguousarray(A['state_mlstm_conv'][0, 16 * c:16 * c + 16].reshape(48, D))
        m["Cs"] = np.ascontiguousarray(A['state_mlstm_C'][0, 16 * c:16 * c + 16])
        m["ns"] = np.ascontiguousarray(A['state_mlstm_n'][0, 16 * c:16 * c + 16].reshape(16, 1024))
        m["ms"] = np.ascontiguousarray(A['state_mlstm_m'][0, 16 * c:16 * c + 16])
        m["maskc"] = np.full((128, 1), float(hf), f32)
        in_maps.append(m)
    if inp.get("_return_in_maps"):
        return in_maps
    res = run_bass_kernel_spmd(nc, in_maps, core_ids=list(range(8)))
    R = res.results
    y_p = np.zeros((4, 2048, D), f32)
    y_s = np.zeros((128, 4, D), f32)
    conv_p = np.zeros((1, 4, 3, D), f32)
    C_p = np.zeros((1, 4, 4, 256, 512), f32)
    n_p = np.zeros((1, 4, 4, 256), f32)
    m_p = np.zeros((1, 4, 4), f32)
    conv_s = np.zeros((1, 128, 3, D), f32)
    C_s = np.zeros((1, 128, 4, 256, 512), f32)
    n_s = np.zeros((1, 128, 4, 256), f32)
    m_s = np.zeros((1, 128, 4), f32)
    v_s = np.zeros((1, 128, 4, D), f32)
    for c in range(8):
        s, hf = c // 2, c % 2
        r = R[c]
        y_p[s, hf * 1024:(hf + 1) * 1024] = r["y_main"]
        y_s[16 * c:16 * c + 16] = r["y_s"].reshape(16, 4, D)
        if hf == 1:
            conv_p[0, s] = r["conv_p"]
            C_p[0, s] = r["C_p"]
            n_p[0, s] = r["n_p"].reshape(4, 256)
            m_p[0, s] = r["m_p"].reshape(4)
        conv_s[0, 16 * c:16 * c + 16] = r["conv_s"].reshape(16, 3, D)
        C_s[0, 16 * c:16 * c + 16] = r["C_s"]
        n_s[0, 16 * c:16 * c + 16] = r["n_s"].reshape(16, 4, 256)
        m_s[0, 16 * c:16 * c + 16] = r["m_s"]
        v_s[0, 16 * c:16 * c + 16] = r["v_s"].reshape(16, 4, D)
    return (y_p, y_s, conv_p, C_p, n_p, m_p, conv_s, C_s, n_s, m_s, v_s)
```

```python
import numpy as np
import concourse.bass as bass
import concourse.mybir as mybir

F32 = mybir.dt.float32
BF16 = mybir.dt.bfloat16
I32 = mybir.dt.int32
AF = mybir.ActivationFunctionType
ALU = mybir.AluOpType
AX = mybir.AxisListType

ENGS = ("pe", "act", "dve", "pool", "sp")
SEM_LIMIT = 3000
DMA_POOL = 6
DMA_LIMIT = 180


def _region(ap):
    t = ap.tensor
    shp = tuple(t.shape)
    row = 1
    for s in shp[1:]:
        row *= s
    off = ap.offset
    p0 = off // row
    f0 = off % row
    dims = list(ap.ap)
    pc = dims[0][1]
    pstep = dims[0][0]
    if pstep == 0:
        p1 = p0 + 1
    else:
        p1 = p0 + (pc - 1) * (pstep // row if pstep >= row else 1) + 1
    ext = 0
    for st, cnt in dims[1:]:
        ext += (cnt - 1) * abs(st)
    f1 = f0 + ext + 1
    if "psum" in str(ap.space).lower():
        bw = 2048 // mybir.dt.size(ap.dtype) if hasattr(mybir.dt, "size") else (512 if ap.dtype == F32 else 1024)
        f0 = (f0 // bw) * bw
        f1 = -(-f1 // bw) * bw
        p0, p1 = 0, 128
    return (p0, p1, f0, f1)


def _overlap(a, b):
    return a[0] < b[1] and b[0] < a[1] and a[2] < b[3] and b[2] < a[3]


def _covers(a, b):
    return a[0] <= b[0] and a[1] >= b[1] and a[2] <= b[2] and a[3] >= b[3]


class _Op:
    __slots__ = ("eng", "fn", "deps", "signal", "sigidx", "is_dma", "dma_slot",
                 "dma_val", "dma_prev", "idx", "name", "tag")


class Sched:
    def __init__(self, nc):
        self.nc = nc
        self.ops = []
        self.track = {}
        self.ndma = {e: 0 for e in ENGS}
        self.tag = ""

    def _is_tracked(self, ap):
        sp = str(ap.space).lower() if hasattr(ap, "space") else ""
        return ("sb" in sp) or ("psum" in sp) or ("state" in sp)

    def op(self, eng, fn, reads=(), writes=(), dma=False, name=None):
        o = _Op()
        o.eng = eng
        o.fn = fn
        o.idx = len(self.ops)
        o.signal = False
        o.sigidx = None
        o.is_dma = dma
        o.name = name
        o.tag = self.tag
        o.dma_prev = None
        deps = set()
        for ap in reads:
            if ap is None or not self._is_tracked(ap):
                continue
            reg = _region(ap)
            lst = self.track.setdefault(ap.tensor.name, [])
            is_ps = "psum" in str(ap.space).lower()
            for r, kind, oi in lst:
                if not _overlap(r, reg):
                    continue
                if kind == "W":
                    deps.add(oi)
                elif is_ps and self.ops[oi].eng != eng:
                    deps.add(oi)
            lst.append([reg, "R", o.idx])
        for ap in writes:
            if ap is None or not self._is_tracked(ap):
                continue
            reg = _region(ap)
            lst = self.track.setdefault(ap.tensor.name, [])
            keep = []
            for rec in lst:
                r, kind, oi = rec
                if oi == o.idx:
                    keep.append(rec)
                    continue
                if _overlap(r, reg):
                    deps.add(oi)
                    if _covers(reg, r):
                        continue
                keep.append(rec)
            keep.append([reg, "W", o.idx])
            self.track[ap.tensor.name] = keep
        deps.discard(o.idx)
        fdeps = []
        for d in deps:
            po = self.ops[d]
            if po.eng == eng and not po.is_dma and not dma and eng == "pe":
                continue
            fdeps.append(d)
        o.deps = fdeps
        if dma:
            k = self.ndma[eng]
            self.ndma[eng] = k + 1
            o.dma_slot = k
        self.ops.append(o)
        return o

    def dma(self, eng, out, in_, **kw):
        return self.op(eng, lambda e: e.dma_start(out=out, in_=in_, **kw),
                       reads=[in_], writes=[out], dma=True)

    def emit(self):
        nc = self.nc
        ops = self.ops
        for o in ops:
            for d in o.deps:
                ops[d].signal = True
        per_eng = {e: [] for e in ENGS}
        cnt = {e: 0 for e in ENGS}
        for o in ops:
            per_eng[o.eng].append(o)
            if o.is_dma:
                continue
            if o.signal:
                cnt[o.eng] += 1
                o.sigidx = cnt[o.eng]
        nsem = {e: max(1, -(-cnt[e] // SEM_LIMIT)) for e in ENGS}
        ndsem = {e: (DMA_POOL * max(1, -(-self.ndma[e] // (DMA_POOL * DMA_LIMIT))) if self.ndma[e] else 0)
                 for e in ENGS}
        import contextlib
        with contextlib.ExitStack() as st:
            sems = {e: [st.enter_context(nc.semaphore(f"s_{e}_{i}")) for i in range(nsem[e])] for e in ENGS}
            dsems = {e: [st.enter_context(nc.semaphore(f"d_{e}_{i}")) for i in range(ndsem[e])] for e in ENGS}
            self.nsems = sum(nsem.values()) + sum(ndsem.values())

            def dma_sem(o):
                k = o.dma_slot
                epoch = k // (DMA_POOL * DMA_LIMIT)
                j = k % DMA_POOL
                nth = (k % (DMA_POOL * DMA_LIMIT)) // DMA_POOL + 1
                return dsems[o.eng][epoch * DMA_POOL + j], 16 * nth

            def comp_sem(o):
                i = o.sigidx - 1
                return sems[o.eng][i // SEM_LIMIT], (i % SEM_LIMIT) + 1

            block = st.enter_context(nc.Block())
            last_dma_on_slot = {}

            def run(engname, handle):
                waited = {e: 0 for e in ENGS}
                dwaited = set()
                dma_hist = []
                for o in per_eng[engname]:
                    need = {}
                    for d in o.deps:
                        po = ops[d]
                        if po.is_dma:
                            if d not in dwaited:
                                dwaited.add(d)
                                s, v = dma_sem(po)
                                handle.wait_ge(s, v)
                        else:
                            if po.sigidx > waited[po.eng]:
                                need[po.eng] = max(need.get(po.eng, 0), po.sigidx)
                    for e, v in need.items():
                        waited[e] = v
                        i = v - 1
                        handle.wait_ge(sems[e][i // SEM_LIMIT], (i % SEM_LIMIT) + 1)
                    if o.is_dma:
                        k = o.dma_slot
                        if k >= DMA_POOL and (k % (DMA_POOL * DMA_LIMIT)) >= DMA_POOL:
                            prev = dma_hist[k - DMA_POOL]
                            if prev.idx not in dwaited:
                                dwaited.add(prev.idx)
                                s, v = dma_sem(prev)
                                handle.wait_ge(s, v)
                        dma_hist.append(o)
                        ins = o.fn(handle)
                        s, v = dma_sem(o)
                        ins.then_inc(s, 16)
                    else:
                        ins = o.fn(handle)
                        if o.signal:
                            s, v = comp_sem(o)
                            ins.then_inc(s, 1)
                if dma_hist:
                    seen = set()
                    for o in reversed(dma_hist):
                        s, v = dma_sem(o)
                        key = id(s)
                        if key in seen:
                            continue
                        seen.add(key)
                        handle.wait_ge(s, v)

            @block.sync
            def _(h):
                run("sp", h)

            @block.scalar
            def _(h):
                run("act", h)

            @block.vector
            def _(h):
                run("dve", h)

            @block.gpsimd
            def _(h):
                run("pool", h)

            @block.tensor
            def _(h):
                run("pe", h)

import contextlib
import math
from concourse.bass_utils import run_bass_kernel_spmd

D = 2048
KC = 16
DFF = 5632
FC = 44
TB = 576
TP = 512
NS = 64
EPS = 1e-6
O_QK, O_VA, O_OA, O_I, O_F, O_UB, O_VB, O_GA, O_GB = 0, 2048, 4096, 6144, 6148, 6152, 8200, 10248, 12296
D_IN = 14344
VN = ['g_ffn1_pre', 'g_ffn1_post', 'g_mix_pre', 'b_conv', 'w_conv0', 'w_conv1', 'w_conv2', 'w_conv3',
      'g_head', 'g_ln_v', 'b_ln_v', 'g_mix_post', 'g_ffn2_pre', 'g_ffn2_post', 'g_ple_pre', 'g_ple_post']
VI = {n: i for i, n in enumerate(VN)}

C_ID, C_TRILT, C_MASKS, C_E, C_SEL, C_SEQ, C_RST, C_CB = 0, 128, 256, 320, 384, 896, 912, 1488
CW = 1496


def make_consts():
    c = np.zeros((128, CW), np.float32)
    c[:, C_ID:C_ID + 128] = np.eye(128, dtype=np.float32)
    s = np.arange(128)[:, None]
    t = np.arange(128)[None, :]
    c[:, C_TRILT:C_TRILT + 128] = (s <= t).astype(np.float32)
    s6 = np.arange(64)[:, None]
    t6 = np.arange(64)[None, :]
    c[:64, C_MASKS:C_MASKS + 64] = ((s6 // 4 == t6 // 4) & (s6 <= t6)).astype(np.float32)
    c[:4, C_E:C_E + 64] = (np.arange(64)[None, :] % 4 == np.arange(4)[:, None]).astype(np.float32)
    for h in range(4):
        c[h, C_SEL + h * 128:C_SEL + (h + 1) * 128] = 1.0
    c[:64, C_SEQ:C_SEQ + 16] = (np.arange(64)[:, None] // 4 == np.arange(16)[None, :]).astype(np.float32)
    r = np.ones(TB, np.float32)
    r[0:512:128] = 0.0
    r[512:576:4] = 0.0
    c[:4, C_RST:C_RST + TB] = r[None, :]
    c[:, C_CB + 0] = EPS
    c[:, C_CB + 1] = -math.log(16.0)
    c[:, C_CB + 2] = 1.0
    c[:, C_CB + 3] = 0.0
    return c


class SX(Sched):
    def mm(self, out, lhsT, rhs, start=True, stop=True, sg=False):
        rd = [lhsT, rhs] + ([] if start else [out])
        if sg:
            return self.op("pe", lambda e: e.matmul(out, lhsT, rhs, start=start, stop=stop, skip_group_check=True),
                           reads=rd, writes=[out])
        return self.op("pe", lambda e: e.matmul(out, lhsT, rhs, start=start, stop=stop), reads=rd, writes=[out])

    def tr(self, out, in_, ident):
        return self.op("pe", lambda e: e.transpose(out, in_, ident), reads=[in_, ident], writes=[out])

    def act(self, out, in_, func, bias=None, scale=None, accum=None):
        kw = {}
        rd = [in_]
        wr = [out]
        if bias is not None:
            kw["bias"] = bias
            if not isinstance(bias, (int, float)):
                rd.append(bias)
        if scale is not None:
            kw["scale"] = scale
            if not isinstance(scale, (int, float)):
                rd.append(scale)
        if accum is not None:
            kw["accum_out"] = accum
            wr.append(accum)
        return self.op("act", lambda e: e.activation(out, in_, func, **kw), reads=rd, writes=wr)

    def tt(self, eng, out, a, b, op):
        return self.op(eng, lambda e: e.tensor_tensor(out, a, b, op), reads=[a, b], writes=[out])

    def ts(self, eng, out, a, s1, s2, op0, op1=None):
        rd = [a] + [x for x in (s1, s2) if x is not None and not isinstance(x, (int, float))]
        if op1 is None:
            return self.op(eng, lambda e: e.tensor_scalar(out, a, s1, None, op0), reads=rd, writes=[out])
        return self.op(eng, lambda e: e.tensor_scalar(out, a, s1, s2, op0, op1), reads=rd, writes=[out])

    def stt(self, eng, out, in0, scalar, in1, op0, op1):
        rd = [in0, in1] + ([] if isinstance(scalar, (int, float)) else [scalar])
        return self.op(eng, lambda e: e.scalar_tensor_tensor(out, in0, scalar, in1, op0, op1), reads=rd, writes=[out])

    def cp(self, eng, out, in_):
        if eng == "act":
            return self.op("act", lambda e: e.copy(out, in_), reads=[in_], writes=[out])
        return self.op(eng, lambda e: e.tensor_copy(out, in_), reads=[in_], writes=[out])

    def memset(self, eng, ap, val):
        return self.op(eng, lambda e: e.memset(ap, val), writes=[ap])

    def recip(self, out, in_):
        return self.op("dve", lambda e: e.reciprocal(out, in_), reads=[in_], writes=[out])

    def rmax(self, out, in_):
        return self.op("dve", lambda e: e.reduce_max(out, in_, AX.X), reads=[in_], writes=[out])

    def rsum(self, out, in_):
        return self.op("dve", lambda e: e.reduce_sum(out, in_, AX.X), reads=[in_], writes=[out])

    def scan(self, out, d0, d1, init, op0, op1):
        rd = [d0, d1] + ([] if isinstance(init, (int, float)) else [init])
        return self.op("dve", lambda e: e.tensor_tensor_scan(out, d0, d1, init, op0, op1), reads=rd, writes=[out])


class _Stop(Exception):
    pass


def build_program(dbg=False, stop_at=None):
    nc = bass.Bass("TRN2", target_bir_lowering=False)

    def din(name, shape, dt=F32):
        return nc.dram_tensor(name, list(shape), dt, kind="ExternalInput").ap()

    def dout(name, shape, dt=F32):
        return nc.dram_tensor(name, list(shape), dt, kind="ExternalOutput").ap()

    xpre = din("xpre", [1024, D])
    xmain = din("xmain", [1024, D])
    pmain = din("pmain", [1024, 256])
    xs = din("xs", [NS, D])
    pss = din("ps", [NS, 256])
    convs = din("convs", [48, D])
    Cs = din("Cs", [16, 4, 256, 512])
    nsd = din("ns", [16, 1024])
    msd = din("ms", [16, 4])
    maskc = din("maskc", [128, 1])
    vecs = din("vecs", [256, 128])
    vrow = din("vrow", [16, D])
    bif = din("bif", [4, 2])
    constd = din("consts", [128, CW])
    w_sp = din("w_spatial", [4, 128, 128])
    b_sp = din("b_spatial", [4, 128])
    W = {}
    for nm, shp in [("w_ffn1_gate", [D, DFF]), ("w_ffn1_up", [D, DFF]), ("w_ffn1_down", [DFF, D]),
                    ("w_in", [D, D_IN]), ("w_a_out", [D, D]), ("w_b_out", [D, D]), ("w_o", [D, D]),
                    ("w_ffn2_gate", [D, DFF]), ("w_ffn2_up", [D, DFF]), ("w_ffn2_down", [DFF, D]),
                    ("w_ple_gate", [D, D]), ("w_ple_up", [256, D])]:
        W[nm] = din(nm, shp)

    y_main = dout("y_main", [1024, D])
    y_s = dout("y_s", [NS, D])
    conv_p = dout("conv_p", [3, D])
    C_p = dout("C_p", [4, 256, 512])
    n_p = dout("n_p", [8, 128])
    m_p = dout("m_p", [4, 1])
    conv_s = dout("conv_s", [48, D])
    C_s = dout("C_s", [16, 4, 256, 512])
    n_s = dout("n_s", [16, 1024])
    m_s = dout("m_s", [16, 4])
    v_s = dout("v_s", [NS, D])

    st = contextlib.ExitStack()
    with st:
        def sb(name, shape, dt):
            return st.enter_context(nc.sbuf_tensor(name, list(shape), dt))

        hT = sb("hT", [128, KC * TB], F32)
        xnT = sb("xnT", [128, KC * TB], BF16)
        G = sb("G", [128, FC * TB], BF16)
        Cst = sb("Cst", [128, 4 * 2 * 512], F32)
        Nst = sb("Nst", [128, 8], F32)
        Mst = sb("Mst", [4, 1], F32)
        Cb = sb("Cb", [128, 2 * 1024], BF16)
        WR = [sb(f"WR{i}", [128, 4096], BF16) for i in range(3)]
        SCR = sb("SCR", [128, 4096], F32)
        CONST = sb("CONST", [128, CW], F32)
        VEC = sb("VEC", [128, 256], F32)
        identb = sb("identb", [128, 128], BF16)
        onesb = sb("onesb", [128, 128], BF16)
        WgT = sb("WgT", [128, 4 * 128], BF16)
        WsT = sb("WsT", [64, 4 * 64], BF16)
        RSTD = sb("RSTD", [128, TB], F32)
        SQ = [sb(f"SQ{i}", [128, TB], BF16) for i in range(3)]
        STM = [sb(f"STM{i}", [128, TB], BF16) for i in range(4)]
        FT = [sb(f"FT{i}", [128, TB], F32) for i in range(2)]
        XP = [sb(f"XP{i}", [128, 640], F32) for i in range(1)]
        HP = sb("HP", [128, KC * 3], F32)
        HS = sb("HS", [128, KC * 48], F32)
        THR = sb("THR", [4, TB], F32)
        WKT = sb("WKT", [128, 5 * 4], F32)
        DCB = sb("DCB", [128, 4 * 20], F32)
        GSM = sb("GSM", [4, 64], F32)
        MKC = sb("MKC", [128, 1], F32)
        BIF = sb("BIF", [4, 4], F32)
        LNS = sb("LNS", [128, 5 * 16], F32)
        PTb = sb("PTb", [128, 2 * 128], BF16)
        NREP = sb("NREP", [128, 2 * 2 * 128], BF16)
        MLS = sb("MLS", [128, 3 * 128], F32)
        SQV = sb("SQV", [128, 512], BF16)
        N0T = sb("N0T", [128, 8 * 16], F32)
        NNT = sb("NNT", [128, 8 * 16], F32)
        QN = sb("QN", [128, 2 * 64], BF16)
        KM = sb("KM", [64, 2 * 256], BF16)
        seqmb = sb("seqmb", [64, 16], BF16)
        PS = st.enter_context(nc.psum_tensor("PS", [128, 4096], F32))

        S = SX(nc)
        MUL, ADD, SUB, MAX = ALU.mult, ALU.add, ALU.subtract, ALU.max

        def bank(b):
            return PS[:, 512 * b:512 * b + 512]

        identf = CONST[:, C_ID:C_ID + 128]
        trilT = CONST[:, C_TRILT:C_TRILT + 128]
        maskS = CONST[0:64, C_MASKS:C_MASKS + 64]
        Emat = CONST[0:4, C_E:C_E + 64]
        seqmask = CONST[0:64, C_SEQ:C_SEQ + 16]
        rst = CONST[0:4, C_RST:C_RST + TB]
        cb_eps = CONST[:, C_CB:C_CB + 1]
        cb_nl16 = CONST[:, C_CB + 1:C_CB + 2]
        cb_one = CONST[:, C_CB + 2:C_CB + 3]

        def sel(h):
            return CONST[0:4, C_SEL + h * 128:C_SEL + (h + 1) * 128]

        def vec(name, c):
            i = VI[name] * 16 + c
            return VEC[:, i:i + 1]

        def hc(c):
            return hT[:, c * TB:(c + 1) * TB]

        def xc(c):
            return xnT[:, c * TB:(c + 1) * TB]

        def gc(c):
            return G[:, c * TB:(c + 1) * TB]

        S.dma("sp", CONST[:], constd)
        S.dma("sp", MKC[:], maskc)
        S.dma("sp", BIF[:, 0:2], bif)
        S.memset("dve", onesb[:], 1.0)
        S.cp("dve", identb[:], identf)
        S.cp("dve", seqmb[:], seqmask)
        S.memset("dve", Cst[:], 0.0)
        S.memset("dve", Nst[:], 0.0)
        S.memset("dve", Mst[:], 0.0)
        S.memset("dve", HP[:], 0.0)
        S.ts("dve", BIF[:, 2:3], BIF[:, 1:2], -1.0, None, MUL)
        for i in range(2):
            slot = SCR[:, i * 128:(i + 1) * 128]
            S.dma("sp", slot, vecs[i * 128:(i + 1) * 128, :])
            S.tr(bank(i)[:, 0:128], slot, identf)
            S.cp("dve", VEC[:, i * 128:(i + 1) * 128], bank(i)[:, 0:128])
        for g in range(4):
            wt = SCR[:, 512 + g * 128:512 + (g + 1) * 128]
            S.dma("sp", wt, w_sp[g])
            S.tr(bank(2 + g % 2)[:, 0:128], wt, identf)
            S.tt("dve", SCR[:, 1024 + g * 128:1024 + (g + 1) * 128], bank(2 + g % 2)[:, 0:128], trilT, MUL)
            S.cp("dve", WgT[:, g * 128:(g + 1) * 128], SCR[:, 1024 + g * 128:1024 + (g + 1) * 128])
            w4 = SCR[0:4, 1600 + g * 4:1600 + g * 4 + 4]
            S.dma("sp", w4, w_sp[g, 0:4, 0:4])
            m1 = bank(5)[0:4, g * 64:(g + 1) * 64]
            S.op("pe", lambda e, m1=m1, w4=w4: e.matmul(m1, w4, Emat, start=True, stop=True), reads=[w4, Emat], writes=[m1])
            m1s = SCR[0:4, 1700 + g * 64:1700 + (g + 1) * 64]
            S.cp("dve", m1s, m1)
            rt = bank(5)[0:64, 256 + g * 64:256 + (g + 1) * 64]
            S.op("pe", lambda e, rt=rt, m1s=m1s: e.matmul(rt, Emat, m1s, start=True, stop=True), reads=[Emat, m1s], writes=[rt])
            S.tt("dve", WsT[:, g * 64:(g + 1) * 64], rt, maskS, MUL)

        state = {"ring": 0, "win": 0, "sq": 0, "stm": 0, "ft": 0, "xp": 0, "xs": 0, "mb": 0, "ev": 0, "hv": 0, "cb": 0, "c0": 0, "psb": 0}

        def nxt(key, n):
            v = state[key]
            state[key] = (v + 1) % n
            return v

        def ev_eng():
            state["ev"] ^= 1
            return "act" if state["ev"] else "dve"

        def load_panel(w_ap, k0c, nk, col0, ncol):
            slot = WR[nxt("ring", 3)]
            src = w_ap[k0c * 128:(k0c + nk) * 128, col0:col0 + ncol].rearrange("(kc p) c -> p kc c", p=128)
            dst = slot[:, 0:nk * ncol].rearrange("p (kc c) -> p kc c", c=ncol)
            S.dma("pool", dst, src)
            return dst

        def fm_group(w_ap, K, col0, ncol, xin, segs, mrows=128):
            nkc = K // 128
            nkp = 16 if nkc == 16 else (11 if nkc == 44 else nkc)
            nm = max(1, ncol // 128)
            mw = min(128, ncol)
            wb = [1024 * nxt("win", 3) for _ in range(nm)]
            ttot = sum(n for _, n in segs)
            for p0 in range(0, nkc, nkp):
                pan = load_panel(w_ap, p0, nkp, col0, ncol)
                for mi in range(nm):
                    for kl in range(nkp):
                        kc = p0 + kl
                        for (c0, n) in segs:
                            o = PS[0:mrows if ncol >= 128 else mw, wb[mi] + c0:wb[mi] + c0 + n]
                            S.mm(o, pan[:, kl, mi * mw:(mi + 1) * mw], xin(kc)[:, c0:c0 + n],
                                 start=(kc == 0), stop=(kc == nkc - 1))
            return [PS[0:(128 if ncol >= 128 else mw), b:b + ttot] for b in wb]

        def tm_group(w_ap, col0, blocks, consume):
            for p0 in (0, 8):
                pan = load_panel(w_ap, p0, 8, col0, 512)
                for bi, (c0, n) in enumerate(blocks):
                    for kl in range(8):
                        kc = p0 + kl
                        S.mm(PS[0:n, 512 * bi:512 * bi + 512], xc(kc)[:, c0:c0 + n], pan[:, kl, :],
                             start=(kc == 0), stop=(kc == 15))
            for bi, (c0, n) in enumerate(blocks):
                consume(bi, n, PS[0:n, 512 * bi:512 * bi + 512])

        STAT = PS[:, 3072:3072 + TB]

        def stats_mm(sq, segs, first, last):
            for (c0, n) in segs:
                S.mm(STAT[:, c0:c0 + n], onesb[:], sq[:, c0:c0 + n], start=first, stop=last)

        def finish_rstd(T, dim):
            ft = FT[nxt("ft", 2)]
            S.act(ft[:, 0:T], STAT[:, 0:T], AF.Sqrt, bias=cb_eps, scale=1.0 / dim)
            S.recip(RSTD[:, 0:T], ft[:, 0:T])

        def prenorm(gname, segs):
            T = sum(n for _, n in segs)
            for c in range(KC):
                sq = SQ[nxt("sq", 3)]
                S.act(sq[:, 0:T], hc(c)[:, 0:T], AF.Square)
                stats_mm(sq, segs, c == 0, c == KC - 1)
            finish_rstd(T, D)
            for c in range(KC):
                S.stt("dve", xc(c)[:, 0:T], hc(c)[:, 0:T], vec(gname, c), RSTD[:, 0:T], MUL, MUL)

        def prenorm_noscale(gname, segs):
            T = sum(n for _, n in segs)
            for c in range(KC):
                sq = SQ[nxt("sq", 3)]
                S.act(sq[:, 0:T], hc(c)[:, 0:T], AF.Square)
                stats_mm(sq, segs, c == 0, c == KC - 1)
                S.act(xc(c)[:, 0:T], hc(c)[:, 0:T], AF.Copy, scale=vec(gname, c))
            finish_rstd(T, D)

        class PostNorm:
            def __init__(self, ybuf, gname, scale, segs):
                self.y, self.g, self.scale, self.segs = ybuf, gname, scale, segs
                self.T = sum(n for _, n in segs)
                self.pend = []
                self.cnt = 0

            def add(self, m, src):
                T = self.T
                sq = SQ[nxt("sq", 3)]
                S.cp("dve", self.y(m)[:, 0:T], src)
                S.act(sq[:, 0:T], self.y(m)[:, 0:T], AF.Square)
                self.pend.append(sq)
                while len(self.pend) > 1:
                    self._flush()

            def _flush(self):
                sq = self.pend.pop(0)
                stats_mm(sq, self.segs, self.cnt == 0, self.cnt == KC - 1)
                self.cnt += 1

            def finish(self):
                T = self.T
                while self.pend:
                    self._flush()
                finish_rstd(T, D)
                for m in range(KC):
                    ft = FT[nxt("ft", 2)]
                    S.stt("dve", ft[:, 0:T], self.y(m)[:, 0:T], vec(self.g, m), RSTD[:, 0:T], MUL, MUL)
                    S.stt("dve", hc(m)[:, 0:T], ft[:, 0:T], float(self.scale), hc(m)[:, 0:T], MUL, ADD)

        pre_x = {"jobs": []}

        def x_jobs(src, blocks):
            return [(src, r0, n, c0, half) for (r0, n, c0) in blocks for half in range(2)]

        def x_slot(i):
            return SCR[:, (i % 4) * 1024:(i % 4 + 1) * 1024]

        def prefetch_x(src, blocks, k=2):
            jobs = x_jobs(src, blocks)[:k]
            for i, (sr, r0, n, c0, half) in enumerate(jobs):
                S.dma("sp", x_slot(i)[0:n, :], sr[r0:r0 + n, half * 1024:(half + 1) * 1024])
            pre_x["jobs"] = jobs

        def load_x(jobs):
            npre = len(pre_x["jobs"])
            assert jobs[:npre] == pre_x["jobs"]
            pre_x["jobs"] = []
            def issue(i):
                sr, r0, n, c0, half = jobs[i]
                S.dma("sp", x_slot(i)[0:n, :], sr[r0:r0 + n, half * 1024:(half + 1) * 1024])
            for i in range(npre, min(4, len(jobs))):
                issue(i)
            for i, (sr, r0, n, c0, half) in enumerate(jobs):
                slot = x_slot(i)
                if i >= 1 and i + 3 < len(jobs) and i + 3 >= 4:
                    issue(i + 3)
                for q in range(2):
                    pb = bank(nxt("mb", 6))
                    for j in range(4):
                        S.tr(pb[:, j * 128:j * 128 + n], slot[0:n, (q * 4 + j) * 128:(q * 4 + j + 1) * 128],
                             identf[0:n, 0:n])
                    c_lo = half * 8 + q * 4
                    o = hT[:, c_lo * TB:(c_lo + 4) * TB].rearrange("p (c t) -> p c t", t=TB)[:, :, c0:c0 + n]
                    i_ = pb.rearrange("p (i t) -> p i t", t=128)[:, :, 0:n]
                    S.cp(ev_eng(), o, i_)

        def store_y(dst, blocks):
            for (r0, n, c0) in blocks:
                for half in range(2):
                    slot = SCR[:, 2048 + state["xs"] * 1024:2048 + state["xs"] * 1024 + 1024]
                    nxt("xs", 2)
                    for q in range(2):
                        pb = bank(nxt("mb", 6))
                        for i in range(4):
                            c = half * 8 + q * 4 + i
                            S.tr(pb[0:n, i * 128:(i + 1) * 128], hc(c)[:, c0:c0 + n], identf)
                        S.cp(ev_eng(), slot[0:n, q * 512:(q + 1) * 512], pb[0:n, :])
                    S.dma("sp", dst[r0:r0 + n, half * 1024:(half + 1) * 1024], slot[0:n, :])

        def ffn(pfx, segs):
            T = sum(n for _, n in segs)
            prenorm_noscale(f"g_{pfx}_pre", segs)
            milestone("ffn prenorm")
            wg, wu, wd = W[f"w_{pfx}_gate"], W[f"w_{pfx}_up"], W[f"w_{pfx}_down"]
            for jp in range(FC // 2):
                if jp == 1:
                    milestone("ffn first pair")
                gw = fm_group(wg, D, 256 * jp, 256, xc, segs)
                sts = []
                for i, w in enumerate(gw):
                    ft = FT[nxt("ft", 2)]
                    S.tt("dve", ft[:, 0:T], w, RSTD[:, 0:T], MUL)
                    stm = STM[nxt("stm", 4)]
                    S.act(stm[:, 0:T], ft[:, 0:T], AF.Silu)
                    S.tt("dve", stm[:, 0:T], stm[:, 0:T], RSTD[:, 0:T], MUL)
                    sts.append(stm)
                uw = fm_group(wu, D, 256 * jp, 256, xc, segs)
                for i, w in enumerate(uw):
                    S.tt("dve", gc(2 * jp + i)[:, 0:T], w, sts[i][:, 0:T], MUL)
            milestone("ffn gate/up")
            pn = PostNorm(xc, f"g_{pfx}_post", 0.5, segs)
            for mp in range(8):
                wins = fm_group(wd, DFF, 256 * mp, 256, gc, segs)
                for i, w in enumerate(wins):
                    pn.add(2 * mp + i, w)
            milestone("ffn down")
            pn.finish()

        GT = lambda i: SCR[0:4, i * TB:(i + 1) * TB]

        def gates(segs, nblk, with_s):
            T = sum(n for _, n in segs)
            w_in = W["w_in"]
            wins = []
            pan = load_panel(w_in, 0, 16, O_I, 8)
            wbase = [1024 * nxt("win", 3) for _ in range(2)]
            for gi in range(2):
                for kc in range(KC):
                    for (c0, n) in segs:
                        S.mm(PS[0:4, wbase[gi] + c0:wbase[gi] + c0 + n], pan[:, kc, gi * 4:(gi + 1) * 4],
                             xc(kc)[:, c0:c0 + n], start=(kc == 0), stop=(kc == KC - 1))
            ips = PS[0:4, wbase[0]:wbase[0] + T]
            fps = PS[0:4, wbase[1]:wbase[1] + T]
            iT, eT, lT, LT, aT, cT, dT = [GT(i)[:, 0:T] for i in range(7)]
            S.ts("dve", iT, ips, BIF[:, 0:1], None, ADD)
            S.act(eT, fps, AF.Exp, bias=BIF[:, 2:3], scale=-1.0)
            S.act(lT, eT, AF.Ln, bias=cb_one[0:4, :], scale=1.0)
            S.scan(LT, rst[:, 0:T], lT, 0.0, MUL, ADD)
            S.tt("dve", aT, iT, LT, ADD)
            A = GSM[:, 0:nblk]
            nL = GSM[:, 4:4 + nblk]
            Mv = GSM[:, 8:8 + nblk]
            cv = GSM[:, 12:12 + nblk]
            MP = GSM[:, 16:16 + nblk]
            dC = GSM[:, 20:20 + nblk]
            S.rmax(A, aT[:, 0:128 * nblk].rearrange("p (b t) -> p b t", t=128))
            S.ts("dve", nL, LT[:, 127:128 * nblk:128], -1.0, None, MUL)
            S.scan(Mv, A, nL, Mst[:, 0:1], MAX, ADD)
            S.tt("dve", cv, Mv, nL, SUB)
            S.cp("dve", MP[:, 0:1], Mst[:, 0:1])
            if nblk > 1:
                S.cp("dve", MP[:, 1:nblk], Mv[:, 0:nblk - 1])
            S.tt("dve", dC, MP, cv, SUB)
            S.act(dC, dC, AF.Exp)
            S.cp("dve", Mst[:, 0:1], Mv[:, nblk - 1:nblk])
            S.cp("dve", cT[:, 0:128 * nblk].rearrange("p (b t) -> p b t", t=128),
                 cv.unsqueeze(2).to_broadcast([4, nblk, 128]))
            if with_s:
                As = GSM[:, 24:40]
                cs = GSM[:, 40:56]
                m0T = SCR[0:4, 7 * TB:7 * TB + 16]
                S.dma("sp", m0T, msd.rearrange("j h -> h j"), allow_slow_non_contiguous=True)
                S.rmax(As, aT[:, 512:576].rearrange("p (j r) -> p j r", r=4))
                S.tt("dve", cs, m0T, As, MAX)
                mnew = SCR[0:4, 7 * TB + 16:7 * TB + 32]
                dCs = SCR[0:4, 7 * TB + 32:7 * TB + 48]
                S.tt("dve", dCs, m0T, cs, SUB)
                S.act(dCs, dCs, AF.Exp)
                S.cp("dve", cT[:, 512:576].rearrange("p (j r) -> p j r", r=4), cs.unsqueeze(2).to_broadcast([4, 16, 4]))
            S.tt("dve", dT, aT, cT, SUB)
            S.act(dT, dT, AF.Exp, bias=cb_nl16[0:4, :], scale=1.0)
            S.tt("dve", eT, LT, cT, SUB)
            S.act(THR[:, 0:T], eT, AF.Exp)
            blocks = [(128 * b, 128) for b in range(nblk)] + ([(512, 64)] if with_s else [])
            pb = bank(nxt("mb", 6))
            for bi, (c0, n) in enumerate(blocks):
                S.tr(pb[0:n, bi * 4:bi * 4 + 4], dT[:, c0:c0 + n], identf[0:4, 0:4])
                S.cp("dve", WKT[0:n, bi * 4:bi * 4 + 4], pb[0:n, bi * 4:bi * 4 + 4])
            pb2 = bank(nxt("mb", 6))
            for h in range(4):
                S.mm(pb2[:, h * 20:h * 20 + nblk], sel(h), dC)
                if with_s:
                    S.mm(pb2[:, h * 20 + 4:h * 20 + 20], sel(h), dCs)
            S.cp("dve", DCB[:], pb2[:, 0:80])
            if with_s:
                S.tt("dve", mnew, cs, LT[:, 512 + 3:576:4], SUB)
                S.dma("sp", m_s.rearrange("j h -> h j"), mnew, allow_slow_non_contiguous=True)

        def conv_chunk(c, win, segs, out, do_conv=True):
            T = sum(n for _, n in segs)
            with_s = len(segs) > 1
            xp = XP[0]
            np_ = segs[0][1]
            S.cp("act", xp[:, 3:3 + np_], win[:, 0:np_])
            S.cp("dve", xp[:, 0:3], HP[:, c * 3:c * 3 + 3])
            S.cp("dve", HP[:, c * 3:c * 3 + 3], xp[:, np_:np_ + 3])
            if with_s:
                xps = xp[:, 520:632].rearrange("p (j r) -> p j r", r=7)
                S.cp("act", xps[:, :, 3:7], win[:, 512:576].rearrange("p (j r) -> p j r", r=4))
                S.cp("dve", xps[:, :, 0:3], HS[:, c * 48:(c + 1) * 48].rearrange("p (j r) -> p j r", r=3))
                S.cp("dve", HS[:, c * 48:(c + 1) * 48].rearrange("p (j r) -> p j r", r=3), xps[:, :, 4:7])
            if not do_conv:
                return
            acc = FT[nxt("ft", 2)]
            S.act(acc[:, 0:np_], xp[:, 0:np_], AF.Identity, bias=vec("b_conv", c), scale=vec("w_conv0", c))
            for j in range(1, 4):
                S.stt("dve", acc[:, 0:np_], xp[:, j:j + np_], vec(f"w_conv{j}", c), acc[:, 0:np_], MUL, ADD)
            if with_s:
                accs = acc[:, 512:576].rearrange("p (j r) -> p j r", r=4)
                S.act(accs, xps[:, :, 0:4], AF.Identity, bias=vec("b_conv", c), scale=vec("w_conv0", c))
                for j in range(1, 4):
                    S.stt("dve", accs, xps[:, :, j:j + 4], vec(f"w_conv{j}", c), accs, MUL, ADD)
            S.act(out[:, 0:T], acc[:, 0:T], AF.Silu)

        def haT(c):
            return G[:, c * TB:(c + 1) * TB]

        def mixin(c):
            return G[:, 9216 + c * TB:9216 + (c + 1) * TB]

        def qkT(i):
            return G[:, 18432 + i * TB:18432 + (i + 1) * TB]

        def vtok(b):
            return G[:, 20736 + b * 512:20736 + (b + 1) * 512]

        def ktok(b):
            return G[:, 23296 + b * 256:23296 + (b + 1) * 256]

        def Ch(h):
            return Cst[:, h * 1024:(h + 1) * 1024]

        def head_proj(h, segs, blocks, full):
            w_in = W["w_in"]
            if full is True:
                qw = fm_group(w_in, D, O_QK + 256 * h, 256, xc, segs)
                for i, w in enumerate(qw):
                    conv_chunk(2 * h + i, w, segs, qkT(i), do_conv=True)
            elif full == "hist":
                qw = fm_group(w_in, D, O_QK + 256 * h, 256, lambda kc: xc(kc)[:, 384:512], [(0, 128)])
                for i, w in enumerate(qw):
                    conv_chunk(2 * h + i, w, [(0, 128)], None, do_conv=False)
            kw_ = fm_group(w_in, D, O_QK + 1024 + 256 * h, 256, xc, segs)
            for i, w in enumerate(kw_):
                conv_chunk(8 + 2 * h + i, w, segs, qkT(2 + i))

            def cons_v(bi, n, ps):
                S.cp(ev_eng(), vtok(bi)[0:n, :], ps)
            tm_group(w_in, O_VA + 512 * h, blocks, cons_v)
            for bi, (c0, n) in enumerate(blocks):
                pk = bank(nxt("mb", 6))
                for dc in range(2):
                    S.mm(pk[0:n, dc * 128:(dc + 1) * 128], qkT(2 + dc)[:, c0:c0 + n], identb[:])
                S.ts("dve", ktok(bi)[0:n, :], pk[0:n, 0:256], WKT[0:n, bi * 4 + h:bi * 4 + h + 1], None, MUL)

        def state_update(h, bi, n, dcol, src_k=None, alt=0):
            pU = PS[:, 1024:2048] if alt == 0 else PS[:, 2048:3072]
            kt = ktok(bi) if src_k is None else src_k
            for dc in range(2):
                S.mm(pU[:, dc * 512:(dc + 1) * 512], kt[0:n, dc * 128:(dc + 1) * 128], vtok(bi)[0:n, :])
            pn_ = PS[:, 3648:3650]
            for dc in range(2):
                S.mm(pn_[:, dc:dc + 1], kt[0:n, dc * 128:(dc + 1) * 128], onesb[0:n, 0:1])
            S.stt("dve", Ch(h), Ch(h), dcol, pU, MUL, ADD)
            S.stt("dve", Nst[:, 2 * h:2 * h + 2], Nst[:, 2 * h:2 * h + 2], dcol, pn_, MUL, ADD)

        def mlstm_out(h, c0, n, small, numps, denps, thr_ps):
            dabs, mx, e2 = MLS[:, 0:n], MLS[:, 128:128 + n], MLS[:, 256:256 + n]
            S.act(dabs, denps, AF.Abs)
            S.tt("dve", mx, dabs, thr_ps, MAX)
            S.stt("dve", e2, mx, EPS, mx, MUL, MUL)
            sqv = SQV[:, 0:4 * n].rearrange("p (v t) -> p v t", t=n)
            S.act(sqv, numps, AF.Square)
            pSS = small[:, 384:384 + n]
            for vc in range(4):
                S.mm(pSS, onesb[:], sqv[:, vc, :], sg=True, start=False, stop=(vc == 3))
            S.stt("dve", dabs, pSS, 1.0 / 512, e2, MUL, ADD)
            S.act(mx, dabs, AF.Sqrt)
            S.recip(e2, mx)
            for vc in range(4):
                S.stt("dve", haT(4 * h + vc)[:, c0:c0 + n], numps[:, vc, :], vec("g_head", 4 * h + vc), e2, MUL, MUL)

        def blk_banks(bi):
            if bi % 2 == 0:
                return bank(0), bank(1), PS[:, 1024:2048]
            return bank(4), bank(5), PS[:, 3072:4096]

        def mlstm_blockA(h, bi, c0):
            n = 128
            small, pN, pU = blk_banks(bi)
            pS = small[:, 0:128]
            for dc in range(2):
                S.mm(pS, qkT(2 + dc)[:, c0:c0 + n], qkT(dc)[:, c0:c0 + n], sg=True, start=(dc == 0), stop=(dc == 1))
            pt = PTb[:, (bi % 2) * 128:(bi % 2) * 128 + 128]
            S.stt("dve", pt, pS, WKT[:, bi * 4 + h:bi * 4 + h + 1], trilT, MUL, MUL)
            S.mm(small[:, 128:256], onesb[:], pt, sg=True, start=True, stop=False)
            S.mm(small[:, 256:384], sel(h), THR[:, c0:c0 + n], sg=True, start=False, stop=True)
            for vc in range(4):
                S.mm(pN[:, vc * 128:(vc + 1) * 128], vtok(bi)[:, vc * 128:(vc + 1) * 128], pt, sg=True, start=(vc == 0), stop=False)
            for dc in range(2):
                S.mm(pU[:, dc * 512:(dc + 1) * 512], ktok(bi)[:, dc * 128:(dc + 1) * 128], vtok(bi)[:, :])
            for dc in range(2):
                S.mm(small[:, dc:dc + 1], ktok(bi)[:, dc * 128:(dc + 1) * 128], onesb[:, 0:1], sg=True, start=False, stop=True)

        def mlstm_blockB(h, bi, c0):
            n = 128
            small, pN, pU = blk_banks(bi)
            dcol = DCB[:, h * 20 + bi:h * 20 + bi + 1]
            cbi = nxt("cb", 2)
            cb = Cb[:, cbi * 1024:(cbi + 1) * 1024]
            S.act(cb, Ch(h), AF.Copy, scale=dcol)
            nrep = NREP[:, cbi * 256:(cbi + 1) * 256]
            for dc in range(2):
                S.ts("dve", nrep[:, dc * 128:(dc + 1) * 128], Nst[:, 2 * h + dc:2 * h + dc + 1].to_broadcast([128, 128]),
                     dcol, None, MUL)
            pD = small[:, 128:256]
            for dc in range(2):
                S.mm(pD, nrep[:, dc * 128:(dc + 1) * 128], qkT(dc)[:, c0:c0 + n], sg=True, start=False, stop=(dc == 1))
            for vc in range(4):
                for dc in range(2):
                    S.mm(pN[:, vc * 128:(vc + 1) * 128], cb[:, dc * 512 + vc * 128:dc * 512 + (vc + 1) * 128],
                         qkT(dc)[:, c0:c0 + n], sg=True, start=False, stop=(dc == 1 and vc == 3))
            S.stt("dve", Ch(h), Ch(h), dcol, pU, MUL, ADD)
            S.stt("dve", Nst[:, 2 * h:2 * h + 2], Nst[:, 2 * h:2 * h + 2], dcol, small[:, 0:2], MUL, ADD)
            mlstm_out(h, c0, n, small, pN.rearrange("p (v t) -> p v t", t=128), pD, small[:, 256:384])

        def mlstm_prompt(h, nblk):
            mlstm_blockA(h, 0, 0)
            for bi in range(nblk):
                if bi + 1 < nblk:
                    mlstm_blockA(h, bi + 1, 128 * (bi + 1))
                mlstm_blockB(h, bi, 128 * bi)

        def mlstm_samples(h):
            c0, n = 512, 64
            bi = 4
            small = bank(0)
            pS = small[0:64, 0:64]
            for dc in range(2):
                S.mm(pS, qkT(2 + dc)[:, c0:c0 + n], qkT(dc)[:, c0:c0 + n], start=(dc == 0), stop=(dc == 1))
            pt = PTb[0:64, 0:64]
            S.stt("dve", pt, pS, WKT[0:64, bi * 4 + h:bi * 4 + h + 1], maskS, MUL, MUL)
            dcs = DCB[:, h * 20 + 4:h * 20 + 20]
            n0s = NNT[:, (2 * h) * 16:(2 * h + 2) * 16].rearrange("p (d j) -> p d j", j=16)
            S.tt("dve", n0s, N0T[:, (2 * h) * 16:(2 * h + 2) * 16].rearrange("p (d j) -> p d j", j=16),
                 dcs.unsqueeze(1).to_broadcast([128, 2, 16]), MUL)
            pD = small[:, 128:128 + n]
            S.mm(pD, onesb[0:64, :], pt, start=True, stop=False)
            for dc in range(2):
                qn = QN[:, dc * 64:(dc + 1) * 64]
                S.tt("dve", qn.rearrange("p (j r) -> p j r", r=4), qkT(dc)[:, c0:c0 + n].rearrange("p (j r) -> p j r", r=4),
                     n0s[:, dc, :].unsqueeze(2).to_broadcast([128, 16, 4]), MUL)
                S.mm(pD, onesb[:], qn, start=False, stop=(dc == 1))
            pN = bank(1)
            for vc in range(4):
                S.mm(pN[:, vc * 64:(vc + 1) * 64], vtok(bi)[0:64, vc * 128:(vc + 1) * 128], pt, start=(vc == 0), stop=False)
            def c0s(j):
                return SCR[:, (j % 4) * 1024:(j % 4 + 1) * 1024]

            def issue_in(j):
                S.dma("pool", c0s(j).rearrange("p (d v) -> p d v", v=512), Cs[j, h].rearrange("(d p) v -> p d v", p=128))
            for j in range(3):
                issue_in(j)
            for j in range(16):
                if j + 3 < 16:
                    issue_in(j + 3)
                c0slot = c0s(j)
                dcol = DCB[:, h * 20 + 4 + j:h * 20 + 5 + j]
                cbi = nxt("cb", 2)
                cb = Cb[:, cbi * 1024:(cbi + 1) * 1024]
                S.act(cb, c0slot, AF.Copy, scale=dcol)
                for vc in range(4):
                    for dc in range(2):
                        S.mm(pN[:, vc * 64 + 4 * j:vc * 64 + 4 * j + 4], cb[:, dc * 512 + vc * 128:dc * 512 + (vc + 1) * 128],
                             qkT(dc)[:, c0 + 4 * j:c0 + 4 * j + 4], start=False, stop=(dc == 1 and vc == 3 and j == 15))
                km = KM[:, (j % 2) * 256:(j % 2) * 256 + 256]
                S.ts("dve", km, ktok(bi)[0:64, :], seqmask[:, j:j + 1], None, MUL)
                pU = PS[:, 1024:2048] if j % 2 == 0 else PS[:, 2048:3072]
                for dc in range(2):
                    S.mm(pU[:, dc * 512:(dc + 1) * 512], km[:, dc * 128:(dc + 1) * 128], vtok(bi)[0:64, :])
                S.stt("dve", c0slot, c0slot, dcol, pU, MUL, ADD)
                S.dma("sp", C_s[j, h].rearrange("(d p) v -> p d v", p=128), c0slot.rearrange("p (d v) -> p d v", v=512))
            thr_ps = small[:, 256:256 + n]
            S.mm(thr_ps, sel(h), THR[:, c0:c0 + n])
            mlstm_out(h, c0, n, small, pN[:, 0:256].rearrange("p (v t) -> p v t", t=64), pD, thr_ps)
            pnn = PS[:, 3648:3648 + 32]
            for dc in range(2):
                S.mm(pnn[:, dc * 16:(dc + 1) * 16], ktok(bi)[0:64, dc * 128:(dc + 1) * 128], seqmb[:])
            S.tt("dve", NNT[:, (2 * h) * 16:(2 * h + 2) * 16], NNT[:, (2 * h) * 16:(2 * h + 2) * 16], pnn, ADD)

        def mixer_state_only(segs, nblk, last):
            blocks = [(128 * b, 128) for b in range(nblk)]
            prenorm("g_mix_pre", segs)
            gates(segs, nblk, False)
            for h in range(4):
                head_proj(h, segs, blocks, full=("hist" if last else "skip"))
                for bi in range(nblk):
                    state_update(h, bi, 128, DCB[:, h * 20 + bi:h * 20 + bi + 1], alt=bi % 2)

        def mixer(segs, nblk, with_s):
            T = sum(n for _, n in segs)
            blocks = [(128 * b, 128) for b in range(nblk)] + ([(512, 64)] if with_s else [])
            w_in = W["w_in"]
            prenorm("g_mix_pre", segs)
            uT = haT

            def LNx(b):
                return G[:, 9216 + b * 2048:9216 + (b + 1) * 2048]
            for cp_ in range(8):
                wins = fm_group(w_in, D, O_UB + 256 * cp_, 256, xc, segs)
                for i, w in enumerate(wins):
                    S.act(uT(2 * cp_ + i)[:, 0:T], w, AF.Gelu)
            VS32 = SCR[0:64, 0:2048]
            S.memset("dve", LNS[:], 0.0)
            for g in range(4):
                def cons_vb(bi, n, ps, g=g):
                    S.act(LNx(bi)[0:n, g * 512:(g + 1) * 512], ps, AF.Gelu, accum=LNS[0:n, bi * 16 + g:bi * 16 + g + 1])
                    junk = FT[nxt("ft", 2)]
                    if bi == 4:
                        S.act(VS32[:, g * 512:(g + 1) * 512], ps, AF.Gelu)
                    S.act(junk[0:n, 0:512], LNx(bi)[0:n, g * 512:(g + 1) * 512], AF.Square,
                          accum=LNS[0:n, bi * 16 + 4 + g:bi * 16 + 5 + g])
                tm_group(w_in, O_VB + 512 * g, blocks, cons_vb)
            for bi, (c0, n) in enumerate(blocks):
                L = lambda k: LNS[0:n, bi * 16 + k:bi * 16 + k + 1]
                S.rsum(L(8), LNS[0:n, bi * 16:bi * 16 + 4])
                S.rsum(L(9), LNS[0:n, bi * 16 + 4:bi * 16 + 8])
                S.ts("dve", L(10), L(8), 1.0 / D, None, MUL)
                S.tt("dve", L(11), L(10), L(10), MUL)
                S.stt("dve", L(12), L(9), 1.0 / D, L(11), MUL, SUB)
                S.act(L(13), L(12), AF.Sqrt, bias=cb_eps[0:n, :], scale=1.0)
                S.recip(L(14), L(13))
                S.stt("dve", L(15), L(10), -1.0, L(14), MUL, MUL)
                if bi == 4:
                    for g in range(4):
                        grep = SCR[0:64, 2048:2560]
                        brep = SCR[0:64, 2560:3072]
                        tmp = SCR[0:64, 3072 + (g % 2) * 512:3584 + (g % 2) * 512]
                        S.dma("sp", grep, vrow[VI["g_ln_v"]:VI["g_ln_v"] + 1, g * 512:(g + 1) * 512].to_broadcast([64, 512]))
                        S.dma("sp", brep, vrow[VI["b_ln_v"]:VI["b_ln_v"] + 1, g * 512:(g + 1) * 512].to_broadcast([64, 512]))
                        S.ts("dve", tmp, VS32[:, g * 512:(g + 1) * 512], L(14), L(15), MUL, ADD)
                        S.tt("dve", tmp, tmp, grep, MUL)
                        S.tt("dve", tmp, tmp, brep, ADD)
                        S.dma("sp", v_s[:, g * 512:(g + 1) * 512], tmp)
                S.ts("dve", LNx(bi)[0:n, :], LNx(bi)[0:n, :], L(14), L(15), MUL, ADD)
            B2 = lambda c: SCR[:, c * 128:(c + 1) * 128]
            B2s = lambda c: SCR[:, 2048 + c * 64:2048 + (c + 1) * 64]
            bsb = SCR[:, 3072:3584]
            bsbs = SCR[:, 3584:3840]
            pw = bank(nxt("mb", 6))
            pws = bank(nxt("mb", 6))
            for g in range(4):
                S.dma("sp", bsb[:, g * 128:(g + 1) * 128], b_sp[g:g + 1, :].to_broadcast([128, 128]))
                S.mm(pw[:, g * 128:(g + 1) * 128], onesb[:], WgT[:, g * 128:(g + 1) * 128])
                if with_s:
                    S.dma("sp", bsbs[:, g * 64:(g + 1) * 64].rearrange("p (j r) -> p j r", r=4),
                          b_sp[g:g + 1, 0:4].unsqueeze(1).to_broadcast([128, 16, 4]))
                    S.mm(pws[:, g * 64:(g + 1) * 64], onesb[0:64, :], WsT[:, g * 64:(g + 1) * 64])
            for c in range(KC):
                g = c // 4
                S.stt("dve", B2(c), pw[:, g * 128:(g + 1) * 128], vec("b_ln_v", c), bsb[:, g * 128:(g + 1) * 128], MUL, ADD)
                if with_s:
                    S.stt("dve", B2s(c), pws[:, g * 64:(g + 1) * 64], vec("b_ln_v", c), bsbs[:, g * 64:(g + 1) * 64], MUL, ADD)
            for c in range(KC):
                g = c // 4
                wb = 1024 * nxt("win", 3)
                for b in range(nblk):
                    S.mm(PS[:, wb + b * 128:wb + (b + 1) * 128], LNx(b)[:, c * 128:(c + 1) * 128], WgT[:, g * 128:(g + 1) * 128])
                if with_s:
                    S.mm(PS[:, wb + 512:wb + 576], LNx(4)[0:64, c * 128:(c + 1) * 128], WsT[:, g * 64:(g + 1) * 64])
                ft = FT[nxt("ft", 2)]
                S.stt("dve", ft[:, 0:128 * nblk].rearrange("p (b t) -> p b t", t=128),
                      PS[:, wb:wb + 128 * nblk].rearrange("p (b t) -> p b t", t=128), vec("g_ln_v", c),
                      B2(c).unsqueeze(1).to_broadcast([128, nblk, 128]), MUL, ADD)
                if with_s:
                    S.stt("dve", ft[:, 512:576], PS[:, wb + 512:wb + 576], vec("g_ln_v", c), B2s(c), MUL, ADD)
                S.tt("dve", uT(c)[:, 0:T], uT(c)[:, 0:T], ft[:, 0:T], MUL)
            for cp_ in range(8):
                gw = fm_group(w_in, D, O_GB + 256 * cp_, 256, xc, segs)
                sts = []
                for i, w in enumerate(gw):
                    stm = STM[nxt("stm", 4)]
                    S.act(stm[:, 0:T], w, AF.Sigmoid)
                    sts.append(stm)
                yw = fm_group(W["w_b_out"], D, 256 * cp_, 256, uT, segs)
                for i, w in enumerate(yw):
                    S.tt("dve", mixin(2 * cp_ + i)[:, 0:T], w, sts[i][:, 0:T], MUL)
            if with_s:
                nrow = SCR[0:16, 2048:3072]
                S.dma("sp", nrow, nsd)
                pb = bank(nxt("mb", 6))
                for k in range(8):
                    S.tr(pb[:, k * 16:(k + 1) * 16], nrow[:, k * 128:(k + 1) * 128], identf[0:16, 0:16])
                S.cp("dve", N0T[:], pb[:, 0:128])
            gates(segs, nblk, with_s)
            for h in range(4):
                head_proj(h, segs, blocks, full=True)
                mlstm_prompt(h, nblk)
                if with_s:
                    mlstm_samples(h)
            if with_s:
                pb = bank(nxt("mb", 6))
                for k in range(4):
                    S.tr(pb[0:16, k * 128:(k + 1) * 128], NNT[:, k * 16:(k + 1) * 16], identf)
                nrow = SCR[0:16, 2048:3072]
                S.cp("dve", nrow[:, 0:512], pb[0:16, :])
                pb2 = bank(nxt("mb", 6))
                for k in range(4):
                    S.tr(pb2[0:16, k * 128:(k + 1) * 128], NNT[:, (4 + k) * 16:(5 + k) * 16], identf)
                S.cp("dve", nrow[:, 512:1024], pb2[0:16, :])
                S.dma("sp", n_s, nrow)
            for cp_ in range(8):
                wins = fm_group(w_in, D, O_OA + 256 * cp_, 256, xc, segs)
                for i, w in enumerate(wins):
                    stm = STM[nxt("stm", 4)]
                    S.act(stm[:, 0:T], w, AF.Sigmoid)
                    S.tt("dve", haT(2 * cp_ + i)[:, 0:T], haT(2 * cp_ + i)[:, 0:T], stm[:, 0:T], MUL)
            for cp_ in range(8):
                gw = fm_group(w_in, D, O_GA + 256 * cp_, 256, xc, segs)
                sts = []
                for i, w in enumerate(gw):
                    stm = STM[nxt("stm", 4)]
                    S.act(stm[:, 0:T], w, AF.Sigmoid)
                    sts.append(stm)
                yw = fm_group(W["w_a_out"], D, 256 * cp_, 256, haT, segs)
                for i, w in enumerate(yw):
                    ft = FT[nxt("ft", 2)]
                    S.tt("dve", ft[:, 0:T], w, sts[i][:, 0:T], MUL)
                    S.tt("dve", mixin(2 * cp_ + i)[:, 0:T], mixin(2 * cp_ + i)[:, 0:T], ft[:, 0:T], ADD)
            pn = PostNorm(xc, "g_mix_post", 1.0, segs)
            for mp in range(8):
                wins = fm_group(W["w_o"], D, 256 * mp, 256, mixin, segs)
                for i, w in enumerate(wins):
                    pn.add(2 * mp + i, w)
            pn.finish()

        def ple(segs, pblocks):
            T = sum(n for _, n in segs)
            prenorm_noscale("g_ple_pre", segs)
            pT = lambda c: G[:, c * TB:(c + 1) * TB]
            for (src, r0, n, c0) in pblocks:
                slot = SCR[:, state["xs"] * 1024:state["xs"] * 1024 + 256]
                nxt("xs", 2)
                S.dma("sp", slot[0:n, :], src[r0:r0 + n, :])
                pb = bank(nxt("mb", 6))
                for i in range(2):
                    S.tr(pb[:, i * 128:i * 128 + n], slot[0:n, i * 128:(i + 1) * 128], identf[0:n, 0:n])
                    S.cp(ev_eng(), pT(i)[:, c0:c0 + n], pb[:, i * 128:i * 128 + n])
            yb = lambda m: G[:, 9216 + m * TB:9216 + (m + 1) * TB]
            pn = PostNorm(yb, "g_ple_post", 1.0, segs)
            for cp_ in range(8):
                gw = fm_group(W["w_ple_gate"], D, 256 * cp_, 256, xc, segs)
                sts = []
                for i, w in enumerate(gw):
                    ft = FT[nxt("ft", 2)]
                    S.tt("dve", ft[:, 0:T], w, RSTD[:, 0:T], MUL)
                    stm = STM[nxt("stm", 4)]
                    S.act(stm[:, 0:T], ft[:, 0:T], AF.Sigmoid)
                    sts.append(stm)
                uw = fm_group(W["w_ple_up"], 256, 256 * cp_, 256, pT, segs)
                for i, w in enumerate(uw):
                    ft = FT[nxt("ft", 2)]
                    S.tt("dve", ft[:, 0:T], w, sts[i][:, 0:T], MUL)
                    pn.add(2 * cp_ + i, ft[:, 0:T])
            pn.finish()

        ms = {"n": 0}

        def milestone(name):
            ms["n"] += 1
            S.tag = f"{ms['n']:02d} {name}"
            if stop_at is not None and ms["n"] >= stop_at:
                print("STOP at milestone", ms["n"], name, flush=True)
                raise _Stop()

        def _program():
            seg_p = [(0, 512)]
            seg_ps = [(0, 512), (512, 64)]
            milestone("setup")
            def pre_blocks(t):
                return [(t * 512 + 128 * b, 128, 128 * b) for b in range(4)]

            def main_jobs(t):
                j = x_jobs(xmain, [(t * 512 + 128 * b, 128, 128 * b) for b in range(4)])
                if t == 1:
                    j = j + x_jobs(xs, [(0, 64, 512)])
                return j
            for t in range(2):
                load_x(x_jobs(xpre, pre_blocks(t)))
                milestone(f"p{t} load_x")
                ffn("ffn1", seg_p)
                milestone(f"p{t} ffn1")
                mixer_state_only(seg_p, 4, last=(t == 1))
                milestone(f"p{t} mixer_state")
                if t == 0:
                    prefetch_x(xpre, pre_blocks(1))
                else:
                    prefetch_x(xmain, [(128 * b, 128, 128 * b) for b in range(4)])
            S.ts("dve", Cst[:], Cst[:], MKC[:, 0:1], None, MUL)
            S.ts("dve", Nst[:], Nst[:], MKC[:, 0:1], None, MUL)
            S.ts("dve", Mst[:], Mst[:], MKC[0:4, 0:1], None, MUL)
            crow = SCR[0:48, 2048:4096]
            S.dma("sp", crow, convs)
            for q in range(4):
                pb = bank(nxt("mb", 6))
                for i in range(4):
                    S.tr(pb[:, i * 48:(i + 1) * 48], crow[:, (q * 4 + i) * 128:(q * 4 + i + 1) * 128], identf[0:48, 0:48])
                S.cp("dve", HS[:, q * 192:(q + 1) * 192], pb[:, 0:192])
            for t in range(2):
                with_s = (t == 1)
                segs = seg_ps if with_s else seg_p
                xb = [(t * 512 + 128 * b, 128, 128 * b) for b in range(4)]
                load_x(main_jobs(t))
                milestone(f"m{t} load_x")
                ffn("ffn1", segs)
                milestone(f"m{t} ffn1")
                mixer(segs, 4, with_s)
                milestone(f"m{t} mixer")
                ffn("ffn2", segs)
                milestone(f"m{t} ffn2")
                pbl = [(pmain, t * 512 + 128 * b, 128, 128 * b) for b in range(4)] + ([(pss, 0, 64, 512)] if with_s else [])
                ple(segs, pbl)
                milestone(f"m{t} ple")
                if t == 0:
                    prefetch_x(xmain, [(512 + 128 * b, 128, 128 * b) for b in range(4)])
                store_y(y_main, xb)
                if with_s:
                    store_y(y_s, [(0, 64, 512)])
                milestone(f"m{t} store")
            for h in range(4):
                S.dma("sp", C_p[h].rearrange("(d p) v -> p d v", p=128), Ch(h).rearrange("p (d v) -> p d v", v=512))
            pb = bank(nxt("mb", 6))
            S.tr(pb[0:8, 0:128], Nst[:, 0:8], identf)
            nrow = SCR[0:8, 0:128]
            S.cp("dve", nrow, pb[0:8, 0:128])
            S.dma("sp", n_p, nrow)
            S.dma("sp", m_p, Mst[:, 0:1])
            pb = bank(nxt("mb", 6))
            for c in range(KC):
                S.tr(pb[0:3, (c % 4) * 128:(c % 4 + 1) * 128], HP[:, c * 3:c * 3 + 3], identf)
                if c % 4 == 3:
                    S.cp("dve", SCR[0:3, 1024 + (c - 3) * 128:1024 + (c + 1) * 128], pb[0:3, :])
                    if c < KC - 1:
                        pb = bank(nxt("mb", 6))
            S.dma("sp", conv_p, SCR[0:3, 1024:1024 + 2048])
            pb = bank(nxt("mb", 6))
            for c in range(KC):
                S.tr(pb[0:48, (c % 4) * 128:(c % 4 + 1) * 128], HS[:, c * 48:(c + 1) * 48], identf)
                if c % 4 == 3:
                    S.cp("dve", SCR[0:48, 0 + (c - 3) * 128:(c + 1) * 128] if False else crow[:, (c - 3) * 128:(c + 1) * 128], pb[0:48, :])
                    if c < KC - 1:
                        pb = bank(nxt("mb", 6))
            S.dma("sp", conv_s, crow)

        try:
            _program()
        except _Stop:
            pass
        S.emit()
        print("program: ops", len(S.ops), "sems", S.nsems, flush=True)
    return nc


_CACHE = {}


def kernel(**inp):
    f32 = np.float32
    A = {k: np.asarray(v) for k, v in inp.items() if not k.startswith("_")}
    if inp.get("_return_in_maps"):
        _CACHE.setdefault("nc", None)
    if "nc" not in _CACHE:
        _CACHE["nc"] = build_program()
    nc = _CACHE["nc"]
    consts = make_consts()
    vecs = np.stack([A['g_ffn1_pre'][0], A['g_ffn1_post'][0], A['g_mix_pre'][0], A['b_conv'][0],
                     A['w_conv'][0, 0], A['w_conv'][0, 1], A['w_conv'][0, 2], A['w_conv'][0, 3],
                     A['g_head'][0], A['g_ln_v'][0], A['b_ln_v'][0], A['g_mix_post'][0], A['g_ffn2_pre'][0],
                     A['g_ffn2_post'][0], A['g_ple_pre'][0], A['g_ple_post'][0]]).astype(f32)
    bif = np.stack([A['b_igate'][0], A['b_fgate'][0]], axis=1).astype(f32)
    shared = {
        "vecs": np.ascontiguousarray(vecs.reshape(256, 128)), "vrow": vecs, "bif": bif, "consts": consts,
        "w_spatial": np.ascontiguousarray(A['w_spatial'][0]), "b_spatial": np.ascontiguousarray(A['b_spatial'][0]),
    }
    for nm in ["w_ffn1_gate", "w_ffn1_up", "w_ffn1_down", "w_in", "w_a_out", "w_b_out", "w_o",
               "w_ffn2_gate", "w_ffn2_up", "w_ffn2_down", "w_ple_gate", "w_ple_up"]:
        shared[nm] = np.ascontiguousarray(A[nm][0])
    in_maps = []
    xp_, pp_ = A['x_prompt'], A['p_prompt'][0]
    for c in range(8):
        s, hf = c // 2, c % 2
        m = dict(shared)
        m["xmain"] = np.ascontiguousarray(xp_[s, hf * 1024:(hf + 1) * 1024])
        m["xpre"] = np.ascontiguousarray(xp_[s, 0:1024]) if hf == 1 else np.zeros((1024, D), f32)
        m["pmain"] = np.ascontiguousarray(pp_[s, hf * 1024:(hf + 1) * 1024])
        m["xs"] = np.ascontiguousarray(A['x_sample'][16 * c:16 * c + 16].reshape(64, D))
        m["ps"] = np.ascontiguousarray(A['p_sample'][0, 16 * c:16 * c + 16].reshape(64, 256))
        m["convs"] = np.ascontiguousarray(A['state_mlstm_conv'][0, 16 * c:16 * c + 16].reshape(48, D))
        m["Cs"] = np.ascontiguousarray(A['state_mlstm_C'][0, 16 * c:16 * c + 16])
        m["ns"] = np.ascontiguousarray(A['state_mlstm_n'][0, 16 * c:16 * c + 16].reshape(16, 1024))
        m["ms"] = np.ascontiguousarray(A['state_mlstm_m'][0, 16 * c:16 * c + 16])
        m["maskc"] = np.full((128, 1), float(hf), f32)
        in_maps.append(m)
    if inp.get("_return_in_maps"):
        return in_maps
    res = run_bass_kernel_spmd(nc, in_maps, core_ids=list(range(8)))
    R = res.results
    y_p = np.zeros((4, 2048, D), f32)
    y_s = np.zeros((128, 4, D), f32)
    conv_p = np.zeros((1, 4, 3, D), f32)
    C_p = np.zeros((1, 4, 4, 256, 512), f32)
    n_p = np.zeros((1, 4, 4, 256), f32)
    m_p = np.zeros((1, 4, 4), f32)
    conv_s = np.zeros((1, 128, 3, D), f32)
    C_s = np.zeros((1, 128, 4, 256, 512), f32)
    n_s = np.zeros((1, 128, 4, 256), f32)
    m_s = np.zeros((1, 128, 4), f32)
    v_s = np.zeros((1, 128, 4, D), f32)
    for c in range(8):
        s, hf = c // 2, c % 2
        r = R[c]
        y_p[s, hf * 1024:(hf + 1) * 1024] = r["y_main"]
        y_s[16 * c:16 * c + 16] = r["y_s"].reshape(16, 4, D)
        if hf == 1:
            conv_p[0, s] = r["conv_p"]
            C_p[0, s] = r["C_p"]
            n_p[0, s] = r["n_p"].reshape(4, 256)
            m_p[0, s] = r["m_p"].reshape(4)
        conv_s[0, 16 * c:16 * c + 16] = r["conv_s"].reshape(16, 3, D)
        C_s[0, 16 * c:16 * c + 16] = r["C_s"]
        n_s[0, 16 * c:16 * c + 16] = r["n_s"].reshape(16, 4, 256)
        m_s[0, 16 * c:16 * c + 16] = r["m_s"]
        v_s[0, 16 * c:16 * c + 16] = r["v_s"].reshape(16, 4, D)
    return (y_p, y_s, conv_p, C_p, n_p, m_p, conv_s, C_s, n_s, m_s, v_s)
```
